# Optimizing a Trainium2 kernel written in Bass

```python
import jax, jax.numpy as jnp
from jax import lax
import numpy as np

D_MODEL = 2048
BATCH = 4
SEQ = 2048
DEPTH = 1

D_MIX = D_MODEL
HG_HEADS = 8
HG_KEY = 128
HG_VAL = 128
HG_KW = HG_HEADS * HG_KEY
HG_WIDTH = HG_HEADS * HG_VAL
ATT_HEADS = 8
ATT_HEAD_DIM = 128
ATT_WIDTH = ATT_HEADS * ATT_HEAD_DIM
DILATED_PATTERNS = ((128, 1), (512, 4), (2048, 16))
N_BACK = 128
ATT_BLOCK = N_BACK
CHUNK = 64
D_FF = 5632
CONV_WIDTH = 3
EPS = 1e-6
IN_SPLITS = (HG_KW, 2 * HG_KW, 2 * HG_KW + HG_WIDTH, 2 * HG_KW + 2 * HG_WIDTH,
             2 * HG_KW + 2 * HG_WIDTH + ATT_WIDTH, 2 * HG_KW + 2 * HG_WIDTH + 2 * ATT_WIDTH)
IN_COLS = 2 * HG_KW + 2 * HG_WIDTH + 3 * ATT_WIDTH

kernel_name = "hymba_hgrn2_dilated_alibi_convffn_adaln"


def _rms(x, w):
    xf = x.astype(jnp.float32)
    y = xf * lax.rsqrt(jnp.mean(xf * xf, axis=-1, keepdims=True) + EPS)
    return (y * w.astype(jnp.float32)).astype(x.dtype)


def _hgrn2(hq, hf, hi, hg, lb, norm_w):
    b, s, _ = hq.shape
    f32 = jnp.float32
    q = jax.nn.silu(hq.astype(f32)).reshape(b, s, HG_HEADS, HG_KEY)
    f = lb.astype(f32) + (1.0 - lb.astype(f32)) * jax.nn.sigmoid(hf.astype(f32))
    f = f.reshape(b, s, HG_HEADS, HG_KEY)
    g = jnp.log(f)
    k = 1.0 - f
    v = hi.astype(f32).reshape(b, s, HG_HEADS, HG_VAL)
    nc = s // CHUNK

    def to_chunks(t):
        return t.reshape(b, nc, CHUNK, HG_HEADS, t.shape[-1]).transpose(1, 0, 3, 2, 4)

    causal = jnp.tril(jnp.ones((CHUNK, CHUNK), dtype=bool))

    def step(state, inp):
        qc, kc, vc, gc = inp
        G = jnp.cumsum(gc, axis=2)
        o_inter = jnp.einsum('bhtk,bhkv->bhtv', qc * jnp.exp(G), state)
        diff = G[:, :, :, None, :] - G[:, :, None, :, :]
        decay = jnp.exp(jnp.where(causal[:, :, None], diff, -jnp.inf))
        A = jnp.einsum('bhtk,bhtsk,bhsk->bhts', qc, decay, kc)
        o_intra = jnp.einsum('bhts,bhsv->bhtv', A, vc)
        G_last = G[:, :, -1:, :]
        state = (jnp.exp(G_last[:, :, 0, :])[..., None] * state
                 + jnp.einsum('bhsk,bhsv->bhkv', kc * jnp.exp(G_last - G), vc))
        return state, o_inter + o_intra

    state0 = jnp.zeros((b, HG_HEADS, HG_KEY, HG_VAL), f32)
    _, o = lax.scan(step, state0, (to_chunks(q), to_chunks(k), to_chunks(v), to_chunks(g)))
    o = o.transpose(1, 0, 3, 2, 4).reshape(b, s, HG_HEADS, HG_VAL)
    o = _rms(o, norm_w).reshape(b, s, HG_WIDTH) * jax.nn.silu(hg.astype(f32))
    return o.astype(hg.dtype)


def _dilated_branch(q, k, v, slopes, dil):
    b, s, h, dh = q.shape
    L = s // dil
    nblk = -(-L // ATT_BLOCK)
    Lp = nblk * ATT_BLOCK

    def split(t):
        t = t.reshape(b, L, dil, h, dh).transpose(0, 2, 3, 1, 4)
        t = jnp.pad(t, ((0, 0), (0, 0), (0, 0), (0, Lp - L), (0, 0)))
        return t.reshape(b, dil, h, nblk, ATT_BLOCK, dh)

    def with_prev(t):
        prev = jnp.pad(t, ((0, 0), (0, 0), (0, 0), (1, 0), (0, 0), (0, 0)))[:, :, :, :-1]
        return jnp.concatenate([prev, t], axis=4)

    qb = split(q)
    kc = with_prev(split(k))
    vc = with_prev(split(v))
    scores = jnp.einsum('brhnqd,brhnkd->brhnqk', qb, kc).astype(jnp.float32) * (dh ** -0.5)

    qi = jnp.arange(ATT_BLOCK)[:, None]
    kj = jnp.arange(2 * ATT_BLOCK)[None, :]
    steps = qi + ATT_BLOCK - kj
    blk = jnp.arange(nblk)[:, None, None]
    valid = (steps >= 0) & (steps <= N_BACK) & ((blk > 0) | (kj >= ATT_BLOCK))
    alibi = -slopes[:, None, None, None] * (steps * dil).astype(jnp.float32)[None, None]
    scores = jnp.where(valid, scores + alibi, -jnp.inf)
    m = jnp.max(scores, axis=-1, keepdims=True)
    p = jnp.exp(scores - m)
    den = jnp.sum(p, axis=-1, keepdims=True)
    o = jnp.einsum('brhnqk,brhnkd->brhnqd', p, vc.astype(jnp.float32)) / den
    lse = (m + jnp.log(den))[..., 0]

    def merge(t):
        t = t.reshape(b, dil, h, Lp, *t.shape[5:])[:, :, :, :L]
        t = jnp.moveaxis(t, 3, 1)
        return t.reshape(b, s, h, *t.shape[4:])

    return merge(o), merge(lse)


def _dilated_mixture(aq, ak, av, q_norm_w, k_norm_w):
    b, s, _ = aq.shape
    q = _rms(aq.reshape(b, s, ATT_HEADS, ATT_HEAD_DIM), q_norm_w)
    k = _rms(ak.reshape(b, s, ATT_HEADS, ATT_HEAD_DIM), k_norm_w)
    v = av.reshape(b, s, ATT_HEADS, ATT_HEAD_DIM)
    slopes = jnp.exp2(-8.0 * jnp.arange(1, ATT_HEADS + 1, dtype=jnp.float32) / ATT_HEADS)
    outs, lses = [], []
    for _, dil in DILATED_PATTERNS:
        o_p, l_p = _dilated_branch(q, k, v, slopes, dil)
        outs.append(o_p)
        lses.append(l_p)
    weights = jax.nn.softmax(jnp.stack(lses), axis=0)
    o = jnp.einsum('pbsh,pbshd->bshd', weights, jnp.stack(outs))
    return o.reshape(b, s, ATT_WIDTH).astype(aq.dtype)


def _causal_dwconv(a, w, bias):
    s = a.shape[1]
    ap = jnp.pad(a, ((0, 0), (CONV_WIDTH - 1, 0), (0, 0)))
    y = bias
    for j in range(CONV_WIDTH):
        y = y + ap[:, j:j + s] * w[j]
    return y


def _layer(x, mod, norm1_w, w_in, lb, hg_norm_w, q_norm_w, k_norm_w, w_out,
           norm2_w, w_up, conv_w, conv_b, w_down):
    shift1, scale1, gate1, shift2, scale2, gate2 = jnp.split(mod, 6, axis=-1)
    h = _rms(x, norm1_w) * (1.0 + scale1[:, None]) + shift1[:, None]
    proj = h @ w_in
    hq, hf, hi, hg, aq, ak, av = jnp.split(proj, IN_SPLITS, axis=-1)
    a_out = _hgrn2(hq, hf, hi, hg, lb, hg_norm_w)
    b_out = _dilated_mixture(aq, ak, av, q_norm_w, k_norm_w)
    mix = jnp.concatenate([a_out, b_out], axis=-1) @ w_out
    x = x + gate1[:, None] * mix

    h2 = _rms(x, norm2_w) * (1.0 + scale2[:, None]) + shift2[:, None]
    u = h2 @ w_up
    a, g = jnp.split(u, 2, axis=-1)
    y = jax.nn.silu(_causal_dwconv(a, conv_w, conv_b)) * g
    return x + gate2[:, None] * (y @ w_down)


def setup_inputs(seed: int = 0) -> dict:
    key = jax.random.key(seed)
    ks = jax.random.split(key, 16)
    f32 = jnp.float32
    nrm = lambda k, shp, sc: jax.random.normal(k, shp, f32) * sc
    D = D_MODEL
    return {
        "x": nrm(ks[0], (BATCH, SEQ, D), 1.0),
        "c": nrm(ks[1], (BATCH, D), 1.0),
        "w_ada": nrm(ks[2], (DEPTH, D, 6 * D), D ** -0.5),
        "b_ada": nrm(ks[3], (DEPTH, 6 * D), 0.02),
        "norm1_w": 1.0 + nrm(ks[4], (DEPTH, D), 0.02),
        "w_in": nrm(ks[5], (DEPTH, D, IN_COLS), D ** -0.5),
        "lb_logits": nrm(ks[6], (DEPTH + 1, HG_KW), 1.0),
        "hg_norm_w": 1.0 + nrm(ks[7], (DEPTH, HG_VAL), 0.02),
        "q_norm_w": 1.0 + nrm(ks[8], (DEPTH, ATT_HEAD_DIM), 0.02),
        "k_norm_w": 1.0 + nrm(ks[9], (DEPTH, ATT_HEAD_DIM), 0.02),
        "w_out": nrm(ks[10], (DEPTH, D_MIX, D), D_MIX ** -0.5),
        "norm2_w": 1.0 + nrm(ks[11], (DEPTH, D), 0.02),
        "w_up": nrm(ks[12], (DEPTH, D, 2 * D_FF), D ** -0.5),
        "conv_w": nrm(ks[13], (DEPTH, CONV_WIDTH, D_FF), CONV_WIDTH ** -0.5),
        "conv_b": nrm(ks[14], (DEPTH, D_FF), 0.02),
        "w_down": nrm(ks[15], (DEPTH, D_FF, D), D_FF ** -0.5),
    }


def reference(x, c, w_ada, b_ada, norm1_w, w_in, lb_logits, hg_norm_w, q_norm_w, k_norm_w,
              w_out, norm2_w, w_up, conv_w, conv_b, w_down):
    lb_all = jnp.cumsum(jax.nn.softmax(lb_logits.astype(jnp.float32), axis=0), axis=0)
    c_act = jax.nn.silu(c)
    for l in range(DEPTH):
        mod = c_act @ w_ada[l] + b_ada[l]
        x = _layer(x, mod, norm1_w[l], w_in[l], lb_all[l], hg_norm_w[l], q_norm_w[l],
                   k_norm_w[l], w_out[l], norm2_w[l], w_up[l], conv_w[l], conv_b[l], w_down[l])
    return x
```

```python
import numpy as np
import concourse.bass as bass
import concourse.mybir as mybir
from concourse.bass_utils import run_bass_kernel_spmd

F32 = mybir.dt.float32
BF16 = mybir.dt.bfloat16
AF = mybir.ActivationFunctionType
ALU = mybir.AluOpType

D = 2048
S = 2048
DFF = 5632
NFF = 44
EPS = 1e-6
SLOPES = [2.0 ** (-8.0 * (h + 1) / 8.0) for h in range(8)]
RANGES = [(0, 512), (512, 896), (896, 1280), (1280, 1664), (1664, 2048)]
QT0 = 896
NQ = 1152
X0 = 1022
NX = 1026
FF_R = [(0, 342), (342, 684), (684, 1026)]
OWN_R = [(2, 514), (514, 1026)]

_DEBUG = None


class Res:
    __slots__ = ("w", "r", "name")

    def __init__(self, name=""):
        self.w = None
        self.r = []
        self.name = name


class Op:
    __slots__ = ("eng", "fn", "deps", "sem", "val", "sig", "pos", "dma")


class Prog:
    ENGS = ["pe", "act", "dve", "pool", "sp"]

    def __init__(self):
        self.ops = {e: [] for e in self.ENGS}
        self.fence = []
        self.dma_sems = {}

    def op(self, eng, fn, rd=(), wr=(), dma_key=None, nofence=False):
        o = Op()
        o.eng = eng
        o.fn = fn
        o.sig = False
        o.dma = dma_key
        deps = []
        for r in rd:
            if r.w is not None:
                deps.append(r.w)
        for w in wr:
            if w.w is not None:
                deps.append(w.w)
            deps.extend(w.r)
        if not nofence:
            deps.extend(self.fence)
        for r in rd:
            r.r.append(o)
        for w in wr:
            w.w = o
            w.r = []
        o.deps = [d for d in deps if d is not o]
        o.pos = len(self.ops[eng])
        self.ops[eng].append(o)
        return o

    def barrier(self):
        f = []
        for e in ["pe", "act", "dve", "sp"]:
            if self.ops[e]:
                f.append(self.ops[e][-1])
        self.fence = f

    def emit(self, nc):
        for e in self.ENGS:
            for o in self.ops[e]:
                for d in o.deps:
                    if d.dma is not None:
                        continue
                    if d.eng == o.eng:
                        if d.eng in ("pe", "pool", "sp"):
                            continue
                        if o.pos - d.pos > 3:
                            continue
                    d.sig = True
        sems = {}
        import contextlib
        stack = contextlib.ExitStack()
        for e in ["pe", "act", "dve"]:
            sems[e] = stack.enter_context(nc.semaphore("cnt_" + e))
        dkeys = []
        for e in self.ENGS:
            for o in self.ops[e]:
                if o.dma is not None and o.dma not in dkeys:
                    dkeys.append(o.dma)
        for k in dkeys:
            sems[("dma", k)] = stack.enter_context(nc.semaphore("dma_%s" % (k,)))
        for e in ["pe", "act", "dve"]:
            c = 0
            for o in self.ops[e]:
                if o.sig:
                    c += 1
                    o.sem = sems[e]
                    o.val = c
        dcnt = {}
        for e in self.ENGS:
            for o in self.ops[e]:
                if o.dma is not None:
                    dcnt[o.dma] = dcnt.get(o.dma, 0) + 16
                    o.sem = sems[("dma", o.dma)]
                    o.val = dcnt[o.dma]
        block = stack.enter_context(nc.Block())

        def run(eng_name, engobj):
            waited = {}
            lst = self.ops[eng_name]
            for o in lst:
                need = {}
                for d in o.deps:
                    if d.dma is None:
                        if d.eng == o.eng:
                            if d.eng in ("pe", "pool", "sp"):
                                continue
                            if o.pos - d.pos > 3:
                                continue
                        if not d.sig:
                            raise RuntimeError("dep not signalled")
                    key = id(d.sem)
                    if key not in need or need[key][1] < d.val:
                        need[key] = (d.sem, d.val)
                for key, (sem, val) in need.items():
                    if waited.get(key, 0) >= val:
                        continue
                    engobj.wait_ge(sem, val)
                    waited[key] = val
                ins = o.fn(engobj)
                if o.dma is not None:
                    ins.then_inc(o.sem, 16)
                elif o.sig:
                    ins.then_inc(o.sem, 1)

        @block.tensor
        def _(t):
            run("pe", t)

        @block.scalar
        def _(s):
            run("act", s)

        @block.vector
        def _(v):
            run("dve", v)

        @block.gpsimd
        def _(g):
            run("pool", g)

        @block.sync
        def _(s):
            run("sp", s)

        stack.close()


class Rot:
    def __init__(self, items):
        self.items = items
        self.i = 0

    def next(self):
        it = self.items[self.i % len(self.items)]
        self.i += 1
        return it


def _interleave(ga, gb, na, nb):
    da = db = False
    ia = ib = 0
    while not (da and db):
        take_a = (not da) and (db or ia * nb <= ib * na)
        if take_a:
            try:
                next(ga)
            except StopIteration:
                da = True
            ia += 1
        else:
            try:
                next(gb)
            except StopIteration:
                db = True
            ib += 1


def build_program(debug=False):
    nc = bass.Bass("TRN2", target_bir_lowering=False)
    P = Prog()

    def din(name, shape):
        return nc.dram_tensor(name, shape, F32, kind="ExternalInput")

    xT_d = din("xT", [128, 16, 2048])
    flag_d = din("flag", [128, 1])
    cT_d = din("cT", [128, 16])
    wada_d = din("wada", [96, 128, 2048])
    bada_d = din("bada", [128, 96])
    n1w_d = din("n1w", [128, 16])
    n2w_d = din("n2w", [128, 16])
    win_d = din("win", [56, 128, 2048])
    lbl_d = din("lbl", [128, 16])
    hgw_d = din("hgw", [128, 1])
    qw_d = din("qw", [128, 1])
    kw_d = din("kw", [128, 1])
    wout_d = din("wout", [16, 128, 2048])
    wup_d = din("wup", [88, 128, 2048])
    convw_d = din("convw", [128, 132])
    convb_d = din("convb", [128, 44])
    wdown_d = din("wdown", [64, 128, 1408])
    etab_d = din("etab", [128, 48 * 128])
    ident_d = din("ident", [128, 128])
    cmask_d = din("cmask", [128, 128])
    rmask_d = din("rmask", [128, 512])
    fmix_d = din("fmix", [128, 384])
    outT_d = nc.dram_tensor("outT", [128, 16, 1024], F32, kind="ExternalOutput")
    dbg_d = None
    if debug:
        dbg_d = nc.dram_tensor("dbg", [128, 16, NQ], F32, kind="ExternalOutput")
        dbg2_d = nc.dram_tensor("dbg2", [128, 16, NX], F32, kind="ExternalOutput")

    A0 = 18432
    cur = [A0]

    def alloc(name, shape, dt, at=None):
        nb = int(np.prod(shape[1:])) * (2 if dt == BF16 else 4)
        nb = (nb + 31) // 32 * 32
        if at is None:
            off = cur[0]
            cur[0] += nb
        else:
            off = at
        t = nc.alloc_sbuf_tensor_at(name, list(shape), dt, offset=off)
        return t, off + nb

    NSLOT = 6
    wring = []
    for k in range(NSLOT):
        t, _ = alloc("wr%d" % k, [128, 2048], BF16)
        wring.append((t, Res("wr%d" % k)))
    etab, _ = alloc("etab", [128, 48, 128], BF16)
    identb, _ = alloc("identb", [128, 128], BF16)
    cmaskb, _ = alloc("cmaskb", [128, 128], BF16)
    fmixb, _ = alloc("fmixb", [128, 384], BF16)
    onesb, _ = alloc("onesb", [128, 128], BF16)
    rmask, _ = alloc("rmask", [128, 512], F32)
    modT, _ = alloc("modT", [128, 96], F32)
    bada, _ = alloc("bada", [128, 96], F32)
    n1w, _ = alloc("n1w", [128, 16], F32)
    n2w, _ = alloc("n2w", [128, 16], F32)
    a1, _ = alloc("a1", [128, 16], F32)
    a1p, _ = alloc("a1p", [128, 16], F32)
    s1p, _ = alloc("s1p", [128, 16], F32)
    a2, _ = alloc("a2", [128, 16], F32)
    cT, _ = alloc("cT", [128, 16], F32)
    ctmp, _ = alloc("ctmp", [128, 16], F32)
    caT, _ = alloc("caT", [128, 16], BF16)
    lbl, _ = alloc("lbl", [128, 16], F32)
    lbv, _ = alloc("lbv", [128, 8], F32)
    oml, _ = alloc("oml", [128, 8], F32)
    hgw, _ = alloc("hgw", [128, 1], F32)
    qw, _ = alloc("qw", [128, 1], F32)
    kw, _ = alloc("kw", [128, 1], F32)
    flag, _ = alloc("flag", [128, 1], F32)
    convw, _ = alloc("convw", [128, 3, 44], F32)
    convb, _ = alloc("convb", [128, 44], F32)
    CONST_END = cur[0]
    R1 = A0 + 24576 + 18432
    assert CONST_END <= R1, (CONST_END, R1)
    R2 = R1 + 65664
    R3 = R2 + 36864
    R3_END = A0 + 210944
    assert R3_END <= 229376

    ps = nc.alloc_psum_tensor("ps", [128, 8, 512], F32)
    bank = [Res("bank%d" % i) for i in range(8)]

    r_const = Res("const")
    r_mod = [Res("mod%d" % i) for i in range(6)]
    r_der = Res("derived")

    def ACT(out, in_, func, rd, wr, bias=None, scale=None):
        kw_ = {}
        if bias is not None:
            kw_["bias"] = bias
        if scale is not None:
            kw_["scale"] = scale
        return P.op("act", lambda e: e.activation(out=out, in_=in_, func=func, **kw_), rd=rd, wr=wr)

    def TT(out, in0, in1, op, rd, wr):
        return P.op("dve", lambda e: e.tensor_tensor(out=out, in0=in0, in1=in1, op=op), rd=rd, wr=wr)

    def TS(out, in0, s1, s2, op0, op1, rd, wr):
        if s2 is None:
            return P.op("dve", lambda e: e.tensor_scalar(out=out, in0=in0, scalar1=s1, scalar2=None, op0=op0), rd=rd, wr=wr)
        return P.op("dve", lambda e: e.tensor_scalar(out=out, in0=in0, scalar1=s1, scalar2=s2, op0=op0, op1=op1), rd=rd, wr=wr)

    def STT(out, in0, scalar, in1, op0, op1, rd, wr):
        return P.op("dve", lambda e: e.scalar_tensor_tensor(out=out, in0=in0, scalar=scalar, in1=in1, op0=op0, op1=op1), rd=rd, wr=wr)

    def CP(out, in_, rd, wr):
        return P.op("dve", lambda e: e.tensor_copy(out=out, in_=in_), rd=rd, wr=wr)

    def MM(out, lhsT, rhs, start, stop, rd, wr, skip=False):
        if skip:
            return P.op("pe", lambda e: e.matmul(out, lhsT=lhsT, rhs=rhs, start=start, stop=stop, skip_group_check=True), rd=rd, wr=wr)
        return P.op("pe", lambda e: e.matmul(out, lhsT=lhsT, rhs=rhs, start=start, stop=stop), rd=rd, wr=wr)

    def DMA(q, out, in_, rd, wr, key, nofence=False):
        return P.op(q, lambda e: e.dma_start(out=out, in_=in_), rd=rd, wr=wr, dma_key=key, nofence=nofence)

    cst = [(modT, None)]
    for (dst, src) in [(bada, bada_d), (n1w, n1w_d), (n2w, n2w_d), (cT, cT_d), (lbl, lbl_d), (hgw, hgw_d),
                       (qw, qw_d), (kw, kw_d), (flag, flag_d), (convb, convb_d), (rmask, rmask_d)]:
        DMA("sp", dst[:], src.ap(), rd=[], wr=[r_const], key="const")
    DMA("sp", convw[:].rearrange("p a b -> p (a b)"), convw_d.ap(), rd=[], wr=[r_const], key="const")
    for (dst, src) in [(identb, ident_d), (cmaskb, cmask_d), (fmixb, fmix_d)]:
        DMA("pool", dst[:], src.ap(), rd=[], wr=[r_const], key="constc")
    for q in range(4):
        DMA("pool", etab[:, 12 * q:12 * q + 12, :].rearrange("p a b -> p (a b)"),
            etab_d.ap()[:, 12 * q * 128:(12 * q + 12) * 128], rd=[], wr=[r_const], key="constc")
    P.op("dve", lambda e: e.memset(onesb[:], 1.0), rd=[], wr=[r_const])
    flagm = fmixb[:, 0:128]
    flagmix = fmixb[:, 128:256]

    wslot = [0]

    def load_w(src_ap, ncols=2048):
        t, r = wring[wslot[0] % NSLOT]
        wslot[0] += 1
        DMA("pool", t[:, 0:ncols], src_ap, rd=[], wr=[r], key="w%d" % ((wslot[0] - 1) % NSLOT), nofence=True)
        return t, r

    ACT(ctmp[:], cT[:], AF.Exp, rd=[r_const], wr=[r_der], scale=-1.0)
    ACT(ctmp[:], ctmp[:], AF.Ln, rd=[], wr=[r_der], bias=1.0)
    ACT(ctmp[:], ctmp[:], AF.Exp, rd=[], wr=[r_der], scale=-1.0)
    TT(caT[:], cT[:], ctmp[:], ALU.mult, rd=[r_const], wr=[r_der])
    TT(lbv[:], lbl[:, 8:16], lbl[:, 0:8], ALU.subtract, rd=[r_const], wr=[r_der])
    ACT(lbv[:], lbv[:], AF.Exp, rd=[], wr=[r_der])
    ACT(lbv[:], lbv[:], AF.Ln, rd=[], wr=[r_der], bias=1.0)
    ACT(lbv[:], lbv[:], AF.Exp, rd=[], wr=[r_der], scale=-1.0)
    TS(oml[:], lbv[:], -1.0, 1.0, ALU.mult, ALU.add, rd=[], wr=[r_der])

    pm_rot = Rot([7])
    xstg = []

    def early_x_loads():
        for oi, ri in enumerate((2, 3, 4)):
            t0, t1 = RANGES[ri]
            st, r_st = xstg[oi]
            for q in range(4):
                DMA("sp", st[:, 4 * q:4 * q + 4, 0:t1 - t0], xT_d.ap()[:, 4 * q:4 * q + 4, t0:t1], rd=[], wr=[r_st],
                    key="st%d" % oi)

    def ada_slot(j):
        t, r = load_w(wada_d.ap()[j])
        b = 7
        for c in range(16):
            MM(ps[:, b, 0:1], t[:, c * 128:(c + 1) * 128], caT[:, c:c + 1], c == 0, c == 15,
               rd=[r, r_der], wr=[bank[b]])
        TT(modT[:, j:j + 1], ps[:, b, 0:1], bada[:, j:j + 1], ALU.add, rd=[r_const], wr=[bank[b], r_mod[j // 16]])

    hT, _ = alloc("hT", [128, 16, 2048], BF16, at=R1)
    r_hT = [Res("hT%d" % i) for i in range(5)]
    stg = []
    t_, _ = alloc("stA", [128, 16, 512], F32, at=R2)
    stg.append((t_, Res("stA")))
    off = R3
    t_, off = alloc("stB", [128, 16, 512], F32, at=off)
    stg.append((t_, Res("stB")))
    t_, off = alloc("stC", [128, 16, 384], F32, at=off)
    stg.append((t_, Res("stC")))
    sqb = []
    for i in range(2):
        t_, off = alloc("sqb%d" % i, [128, 512], BF16, at=off)
        sqb.append((t_, Res("sqb%d" % i)))
    rstdB, off = alloc("rstdB", [128, 512], F32, at=off)
    r_rstd = Res("rstd")
    assert off <= R3_END
    sq_rot = Rot(sqb)
    xstg.extend(stg)
    early_x_loads()
    r_hx = [Res("hx%d" % i) for i in range(5)]
    def phaseB_range(oi, ri):
        t0, t1 = RANGES[ri]
        n = t1 - t0
        st, r_st = stg[oi % 3]
        b = oi % 2
        if oi >= 3:
            for q in range(4):
                DMA("sp", st[:, 4 * q:4 * q + 4, 0:n], xT_d.ap()[:, 4 * q:4 * q + 4, t0:t1], rd=[], wr=[r_st],
                    key="st%d" % (oi % 3))
        for c in range(16):
            sq, r_sq = sq_rot.next()
            ACT(sq[:, 0:n], st[:, c, 0:n], AF.Square, rd=[r_st], wr=[r_sq])
            MM(ps[:, b, 0:n], onesb[:], sq[:, 0:n], c == 0, c == 15, rd=[r_sq, r_const], wr=[bank[b]])
        ACT(rstdB[:, 0:n], ps[:, b, 0:n], AF.Ln, rd=[], wr=[bank[b], r_rstd], bias=EPS, scale=1.0 / D)
        ACT(rstdB[:, 0:n], rstdB[:, 0:n], AF.Exp, rd=[], wr=[r_rstd], scale=-0.5)
        for c in range(16):
            TT(hT[:, c, t0:t1], st[:, c, 0:n], rstdB[:, 0:n], ALU.mult, rd=[r_st, r_rstd], wr=[r_hx[ri]])
    for oi, ri in enumerate((2, 3, 4)):
        phaseB_range(oi, ri)
    for j in range(0, 6):
        ada_slot(j)
    phaseB_range(3, 0)
    for j in range(6, 12):
        ada_slot(j)
    phaseB_range(4, 1)
    for j in range(12, 32):
        ada_slot(j)
    TS(a1[:], modT[:, 16:32], 1.0, None, ALU.add, None, rd=[r_mod[1]], wr=[r_der])
    TT(a1[:], a1[:], n1w[:], ALU.mult, rd=[r_const], wr=[r_der])
    TS(a1p[:], a1[:], flag[:, 0:1], None, ALU.mult, None, rd=[r_const], wr=[r_der])
    TS(s1p[:], modT[:, 0:16], flag[:, 0:1], None, ALU.mult, None, rd=[r_mod[0], r_const], wr=[r_der])

    for c in range(16):
        if c % 2 == 0:
            ACT(hT[:, c, 0:1024], hT[:, c, 0:1024], AF.Identity, rd=[r_hx[0], r_hx[1], r_hx[2], r_der],
                wr=[r_hT[0], r_hT[1], r_hT[2]], bias=s1p[:, c:c + 1], scale=a1p[:, c:c + 1])
            TS(hT[:, c, 1024:2048], hT[:, c, 1024:2048], a1[:, c:c + 1], modT[:, c:c + 1], ALU.mult, ALU.add,
               rd=[r_hx[2], r_hx[3], r_hx[4], r_der, r_mod[0]], wr=[r_hT[2], r_hT[3], r_hT[4]])
        else:
            TS(hT[:, c, 0:1024], hT[:, c, 0:1024], a1p[:, c:c + 1], s1p[:, c:c + 1], ALU.mult, ALU.add,
               rd=[r_hx[0], r_hx[1], r_hx[2], r_der], wr=[r_hT[0], r_hT[1], r_hT[2]])
            ACT(hT[:, c, 1024:2048], hT[:, c, 1024:2048], AF.Identity, rd=[r_hx[2], r_hx[3], r_hx[4], r_der, r_mod[0]],
                wr=[r_hT[2], r_hT[3], r_hT[4]], bias=modT[:, c:c + 1], scale=a1[:, c:c + 1])
    P.barrier()

    mixT, _ = alloc("mixT", [128, 16, NQ], BF16, at=R2)
    r_mix = [Res("mix%d" % i) for i in range(16)]
    ada_next = [32]

    def ada_some(k):
        j0 = ada_next[0]
        k = min(k, 96 - j0)
        if k <= 0:
            return
        assert j0 // 16 == (j0 + k - 1) // 16
        b = 7
        for i in range(k):
            t, r = load_w(wada_d.ap()[j0 + i])
            for c in range(16):
                MM(ps[:, b, i:i + 1], t[:, c * 128:(c + 1) * 128], caT[:, c:c + 1], c == 0, c == 15,
                   rd=[r, r_der], wr=[bank[b]], skip=True)
        TT(modT[:, j0:j0 + k], ps[:, b, 0:k], bada[:, j0:j0 + k], ALU.add, rd=[r_const], wr=[bank[b], r_mod[j0 // 16]])
        ada_next[0] += k

    pb_rot = Rot([0, 1, 2])
    pending = []

    def flush_pending():
        while pending:
            pending.pop(0)()

    def inproj_group(slot_t, slot_r, ri):
        t0, t1 = RANGES[ri]
        n = t1 - t0
        b = pb_rot.next()
        for c in range(16):
            MM(ps[:, b, 0:n], slot_t[:, c * 128:(c + 1) * 128], hT[:, c, t0:t1], c == 0, c == 15,
               rd=[slot_r, r_hT[ri]], wr=[bank[b]])
            if c % 4 == 3 and c != 15:
                if pending:
                    pending.pop(0)()
                yield
        return b, n, t0

    off = R3
    H = {}

    def hal(name, shape, dt, nbuf=1):
        nonlocal off
        lst = []
        for i in range(nbuf):
            t_, off = alloc("h_%s%d" % (name, i), shape, dt, at=off)
            lst.append((t_, Res("h_%s%d" % (name, i))))
        H[name] = lst

    hal("kb", [128, 2048], BF16)
    hal("ka", [128, 18, 160], BF16, 2)
    hal("qd", [128, NQ], BF16, 2)
    hal("qa", [128, NQ], BF16, 2)
    hal("q1x", [128, 18, 32], BF16, 2)
    hal("gate", [128, NQ], BF16, 2)
    hal("decL", [128, 32], F32, 2)
    hal("vtok", [128, 16, 128], BF16)
    hal("ktok", [128, 16, 128], BF16)
    hal("qf", [128, NQ], BF16)
    hal("kkb", [128, 512], BF16)
    hal("hiT", [128, 2048], BF16)
    hal("oT", [128, 384], F32)
    hal("t1", [128, 512], F32)
    hal("t2", [128, 512], F32)
    hal("t3", [128, 512], F32)
    hal("t4", [128, 512], BF16)
    hal("t5", [128, 512], BF16)
    hal("sqh", [128, 384], BF16)
    hal("rst", [128, 384], F32)
    hal("state", [128, 128], F32)
    hal("Sb", [128, 128], BF16, 2)
    hal("At", [128, 64], BF16, 2)
    assert off <= R3_END, off - R3
    for (ka_, r_ka_) in H["ka"]:
        P.op("dve", lambda e, k_=ka_: e.memset(k_[:, :, 32:64], 0.0), rd=[], wr=[r_ka_])
        P.op("dve", lambda e, k_=ka_: e.memset(k_[:, :, 128:160], 0.0), rd=[], wr=[r_ka_])

    def sigmoid_chain(dst, src_ps, n, bank_res, rd_extra=()):
        (d, r_d) = dst
        ACT(d[:, 0:n], src_ps, AF.Exp, rd=list(rd_extra), wr=[bank_res, r_d], scale=-1.0)
        ACT(d[:, 0:n], d[:, 0:n], AF.Ln, rd=[], wr=[r_d], bias=1.0)
        ACT(d[:, 0:n], d[:, 0:n], AF.Exp, rd=[], wr=[r_d], scale=-1.0)

    def hgrn_inproj(h, par, part):
        kb, r_kb = H["kb"][0]
        ka, r_ka = H["ka"][par]
        qd, r_qd = H["qd"][par]
        qa, r_qa = H["qa"][par]
        q1x, r_q1x = H["q1x"][par]
        gate, r_gate = H["gate"][par]
        decL, r_decL = H["decL"][par]
        qf, r_qf = H["qf"][0]
        hiT, r_hi = H["hiT"][0]
        t1, r_t1 = H["t1"][0]
        t2, r_t2 = H["t2"][0]
        t3, r_t3 = H["t3"][0]
        t4, r_t4 = H["t4"][0]
        t5, r_t5 = H["t5"][0]
        kkb, r_kkb = H["kkb"][0]
        base = 4 * h
        if part == 0:
            st_, sr_ = load_w(win_d.ap()[base + 0])
        for ri in ((2, 3, 4) if part == 0 else ()):
            b, n, t0 = yield from inproj_group(st_, sr_, ri)
            qc = t0 - QT0
            ACT(t1[:, 0:n], ps[:, b, 0:n], AF.Copy, rd=[], wr=[bank[b], r_t1])
            flush_pending()
            ACT(t2[:, 0:n], t1[:, 0:n], AF.Exp, rd=[r_t1], wr=[r_t2], scale=-1.0)
            ACT(t2[:, 0:n], t2[:, 0:n], AF.Ln, rd=[], wr=[r_t2], bias=1.0)
            ACT(t2[:, 0:n], t2[:, 0:n], AF.Exp, rd=[], wr=[r_t2], scale=-1.0)
            TT(qf[:, qc:qc + n], t1[:, 0:n], t2[:, 0:n], ALU.mult, rd=[r_t1, r_t2], wr=[r_qf])
            yield
        if part == 0:
            st_, sr_ = load_w(win_d.ap()[base + 2])
        for ri in (range(5) if part == 0 else ()):
            b, n, t0 = yield from inproj_group(st_, sr_, ri)
            nch = n // 64
            n32 = n // 32
            c0 = t0 // 64
            gv = t2[:, 0:n].rearrange("p (c t) -> p c t", t=64)
            dv = t3[:, 0:n].rearrange("p (c t) -> p c t", t=64)
            kkv = kkb[:, 0:n].rearrange("p (c t) -> p c t", t=64)
            ACT(t1[:, 0:n], ps[:, b, 0:n], AF.Exp, rd=[], wr=[bank[b], r_t1], scale=-1.0)
            flush_pending()
            ACT(t2[:, 0:n], t1[:, 0:n], AF.Ln, rd=[r_t1], wr=[r_t2], bias=1.0)
            ACT(t2[:, 0:n], t2[:, 0:n], AF.Exp, rd=[], wr=[r_t2], scale=-1.0)
            ACT(t3[:, 0:n], t2[:, 0:n], AF.Ln, rd=[r_t2, r_der], wr=[r_t3],
                bias=lbv[:, h:h + 1], scale=oml[:, h:h + 1])

            def p1(n=n):
                STT(kkb[:, 0:n], t1[:, 0:n], oml[:, h:h + 1], t2[:, 0:n], ALU.mult, ALU.mult,
                    rd=[r_t1, r_t2, r_der], wr=[r_kkb])
                P.op("dve", lambda e, o_=t2[:, 0:n], d0=rmask[:, 0:n], d1=t3[:, 0:n]: e.tensor_tensor_scan(
                    out=o_, data0=d0, data1=d1, initial=0.0, op0=ALU.mult, op1=ALU.add),
                    rd=[r_t3, r_const], wr=[r_t2])

            def p2(n=n, nch=nch, c0=c0, gv=gv, dv=dv, t0=t0):
                ACT(decL[:, c0:c0 + nch], gv[:, :, 63], AF.Exp, rd=[r_t2], wr=[r_decL])
                TT(dv, gv, gv[:, :, 63:64].to_broadcast([128, nch, 64]), ALU.subtract, rd=[r_t2], wr=[r_t3])
                ACT(t4[:, 0:n], t3[:, 0:n], AF.Exp, rd=[r_t3], wr=[r_t4], scale=-1.0)
                TT(kb[:, t0:t0 + n], kkb[:, 0:n], t4[:, 0:n], ALU.mult, rd=[r_kkb, r_t4], wr=[r_kb])

            pieces = [p1, p2]
            if ri >= 2:
                qc = t0 - QT0
                oc0 = qc // 64
                qfv = qf[:, qc:qc + n].rearrange("p (c t) -> p c t", t=64)
                g32 = t2[:, 0:n].rearrange("p (c t) -> p c t", t=32)
                d32 = t3[:, 0:n].rearrange("p (c t) -> p c t", t=32)

                def p3(n=n, qc=qc, n32=n32, g32=g32, d32=d32):
                    ACT(t5[:, 0:n], t2[:, 0:n], AF.Exp, rd=[r_t2], wr=[r_t5])
                    TT(qa[:, qc:qc + n], qf[:, qc:qc + n], t5[:, 0:n], ALU.mult, rd=[r_t5, r_qf], wr=[r_qa])
                    TT(d32, g32, g32[:, :, 15:16].to_broadcast([128, n32, 32]), ALU.subtract, rd=[r_t2], wr=[r_t3])

                def p4(n=n, qc=qc, oc0=oc0, nch=nch):
                    ACT(t4[:, 0:n], t3[:, 0:n], AF.Exp, rd=[r_t3], wr=[r_t4])
                    TT(qd[:, qc:qc + n], qf[:, qc:qc + n], t4[:, 0:n], ALU.mult, rd=[r_t4, r_qf], wr=[r_qd])
                    ACT(t5[:, 0:n], t3[:, 0:n], AF.Exp, rd=[r_t3], wr=[r_t5], scale=-1.0)
                    ka_diag = ka[:, oc0:oc0 + nch, 0:128].rearrange("p c (two x) -> p c two x", two=2)[:, :, :, 0:32]
                    TT(ka_diag, kkb[:, 0:n].rearrange("p (c two x) -> p c two x", two=2, x=32),
                       t5[:, 0:n].rearrange("p (c two x) -> p c two x", two=2, x=32), ALU.mult,
                       rd=[r_kkb, r_t5], wr=[r_ka])

                def p5(n=n, oc0=oc0, nch=nch, gv=gv, dv=dv, qfv=qfv, kkv=kkv):
                    TT(dv, gv, gv[:, :, 31:32].to_broadcast([128, nch, 64]), ALU.subtract, rd=[r_t2], wr=[r_t3])
                    h4 = t4[:, 0:n // 2].rearrange("p (c t) -> p c t", t=32)
                    h5 = t5[:, 0:n // 2].rearrange("p (c t) -> p c t", t=32)
                    ACT(h4, dv[:, :, 32:64], AF.Exp, rd=[r_t3], wr=[r_t4])
                    TT(q1x[:, oc0:oc0 + nch, :], qfv[:, :, 32:64], h4, ALU.mult, rd=[r_t4, r_qf], wr=[r_q1x])
                    ACT(h5, dv[:, :, 0:32], AF.Exp, rd=[r_t3], wr=[r_t5], scale=-1.0)
                    TT(ka[:, oc0:oc0 + nch, 96:128], kkv[:, :, 0:32], h5, ALU.mult, rd=[r_kkb, r_t5], wr=[r_ka])

                pieces += [p3, p4, p5]
            pending.extend(pieces)
            yield
        if part == 0:
            st_, sr_ = load_w(win_d.ap()[base + 1])
        for ri in ((2, 3, 4) if part == 0 else ()):
            b, n, t0 = yield from inproj_group(st_, sr_, ri)
            qc = t0 - QT0
            ACT(t1[:, 0:n], ps[:, b, 0:n], AF.Copy, rd=[], wr=[bank[b], r_t1])
            flush_pending()
            ACT(t2[:, 0:n], t1[:, 0:n], AF.Exp, rd=[r_t1], wr=[r_t2], scale=-1.0)
            ACT(t2[:, 0:n], t2[:, 0:n], AF.Ln, rd=[], wr=[r_t2], bias=1.0)
            ACT(t2[:, 0:n], t2[:, 0:n], AF.Exp, rd=[], wr=[r_t2], scale=-1.0)
            TT(gate[:, qc:qc + n], t1[:, 0:n], t2[:, 0:n], ALU.mult, rd=[r_t1, r_t2], wr=[r_gate])
            yield
        if part == 0:
            st_, sr_ = load_w(win_d.ap()[base + 3])
        for ri in (range(5) if part == 0 else ()):
            b, n, t0 = yield from inproj_group(st_, sr_, ri)
            ACT(hiT[:, t0:t0 + n], ps[:, b, 0:n], AF.Copy, rd=[], wr=[bank[b], r_hi])
            yield
        flush_pending()
    def hgrn_transposes(h, par):
        kb, r_kb = H["kb"][0]
        hiT, r_hi = H["hiT"][0]
        vtok, r_v = H["vtok"][0]
        ktok, r_k = H["ktok"][0]
        tb_rot = Rot([3, 4])
        for (src, r_src, dst, r_dst) in ((kb, r_kb, ktok, r_k), (hiT, r_hi, vtok, r_v)):
            for g in range(4):
                b = tb_rot.next()
                for i in range(4):
                    tl = 4 * g + i
                    MM(ps[:, b, i * 128:(i + 1) * 128], src[:, tl * 128:(tl + 1) * 128], identb[:], True, True,
                       rd=[r_src, r_const], wr=[bank[b]])
                CP(dst[:, 4 * g:4 * g + 4, :].rearrange("p a b -> p (a b)"), ps[:, b, :], rd=[], wr=[bank[b], r_dst])
            yield

    def hgrn_recur(h, par):
        ka, r_ka = H["ka"][par]
        qd, r_qd = H["qd"][par]
        qa, r_qa = H["qa"][par]
        q1x, r_q1x = H["q1x"][par]
        gate, r_gate = H["gate"][par]
        decL, r_decL = H["decL"][par]
        vtok, r_v = H["vtok"][0]
        ktok, r_k = H["ktok"][0]
        oT, r_oT = H["oT"][0]
        sqh, r_sqh = H["sqh"][0]
        rst, r_rst = H["rst"][0]
        state, r_state = H["state"][0]
        bEO = [3, 4]
        bUU = [6, 7]
        bO, bR = 5, 6
        P.op("dve", lambda e: e.memset(state[:], 0.0), rd=[], wr=[r_state])

        def emit_A(ci):
            oc = ci - 14
            qc = ci * 64 - QT0
            pb_ = 64 * (ci % 2)
            bA = bEO[ci % 2]
            At, r_At = H["At"][ci % 2]
            MM(ps[pb_:pb_ + 64, bA, 0:32], ka[:, oc, 0:64], qd[:, qc:qc + 32], True, True,
               rd=[r_ka, r_qd], wr=[bank[bA]], skip=True)
            MM(ps[pb_:pb_ + 64, bA, 32:64], ka[:, oc, 32:96], qd[:, qc + 32:qc + 64], True, False,
               rd=[r_ka, r_qd], wr=[bank[bA]], skip=True)
            MM(ps[pb_:pb_ + 64, bA, 32:64], ka[:, oc, 96:160], q1x[:, oc, :], False, True,
               rd=[r_ka, r_q1x], wr=[bank[bA]], skip=True)
            TT(At[pb_:pb_ + 64, 0:64], ps[pb_:pb_ + 64, bA, 0:64], cmaskb[pb_:pb_ + 64, pb_:pb_ + 64],
               ALU.mult, rd=[r_const], wr=[bank[bA], r_At])

        for ci in range(32):
            tile_i = ci // 2
            pb_ = 64 * (ci % 2)
            if ci == 13:
                emit_A(14)
            if ci + 1 <= 31 and ci + 1 >= 15:
                emit_A(ci + 1)
            if ci < 31:
                bU = bUU[ci % 2]
                MM(ps[:, bU, 128:256], ktok[pb_:pb_ + 64, tile_i, :], vtok[pb_:pb_ + 64, tile_i, :], True, True,
                   rd=[r_k, r_v], wr=[bank[bU]], skip=True)
            if ci >= 14:
                qc = ci * 64 - QT0
                rloc = (qc % 384)
                At, r_At = H["At"][ci % 2]
                Sb, r_Sb = H["Sb"][ci % 2]
                MM(ps[:, bO, rloc:rloc + 64], Sb[:], qa[:, qc:qc + 64], True, False, rd=[r_Sb, r_qa], wr=[bank[bO]])
                MM(ps[:, bO, rloc:rloc + 64], vtok[pb_:pb_ + 64, tile_i, :], At[pb_:pb_ + 64, 0:64], False, True,
                   rd=[r_v, r_At], wr=[bank[bO]])
            if ci < 31:
                STT(state[:], state[:], decL[:, ci:ci + 1], ps[:, bU, 128:256], ALU.mult, ALU.add,
                    rd=[r_decL], wr=[bank[bU], r_state])
                if ci + 1 >= 14:
                    Sbn, r_Sbn = H["Sb"][(ci + 1) % 2]
                    CP(Sbn[:], state[:], rd=[r_state], wr=[r_Sbn])
            if ci >= 14 and (ci * 64 - QT0) % 384 == 320:
                q0 = (ci * 64 - QT0) - 320
                ACT(sqh[:], ps[:, bO, 0:384], AF.Square, rd=[], wr=[bank[bO], r_sqh])
                ACT(oT[:], ps[:, bO, 0:384], AF.Copy, rd=[], wr=[bank[bO], r_oT])
                MM(ps[:, bR, 0:384], onesb[:], sqh[:], True, True, rd=[r_sqh, r_const], wr=[bank[bR]])
                ACT(rst[:], ps[:, bR, 0:384], AF.Ln, rd=[], wr=[bank[bR], r_rst], bias=EPS, scale=1.0 / 128)
                ACT(rst[:], rst[:], AF.Exp, rd=[], wr=[r_rst], scale=-0.5)
                STT(oT[:], oT[:], hgw[:, 0:1], rst[:], ALU.mult, ALU.mult, rd=[r_rst, r_const], wr=[r_oT])
                TT(mixT[:, h, q0:q0 + 384], oT[:], gate[:, q0:q0 + 384], ALU.mult, rd=[r_oT, r_gate], wr=[r_mix[h]])
            yield

    AT = {}

    def attn_alloc():
        nonlocal off
        off = R3

        def aal(name, shape, dt, nbuf=1):
            nonlocal off
            lst = []
            for i in range(nbuf):
                t_, off = alloc("a_%s%d" % (name, i), shape, dt, at=off)
                lst.append((t_, Res("a_%s%d" % (name, i))))
            AT[name] = lst

        aal("qn", [128, NQ], BF16, 2)
        aal("kn", [128, 2048], BF16, 2)
        aal("rawq", [128, NQ], F32)
        aal("rawk", [128, 2048], F32)
        aal("avT", [128, 2048], BF16)
        aal("vt", [128, 48, 128], BF16)
        aal("sq", [128, 512], BF16, 2)
        aal("rs", [128, 512], F32)
        aal("ex", [128, 512], BF16, 3)
        aal("pT", [128, 512], BF16, 3)
        aal("rden", [128, 384], F32)
        aal("qws", [128, 1], F32)
        assert off <= R3_END, off - R3

    def attn_inproj(h, par):
        qn, r_qn = AT["qn"][par]
        kn, r_kn = AT["kn"][par]
        rawq, r_rq = AT["rawq"][0]
        rawk, r_rk = AT["rawk"][0]
        avT, r_av = AT["avT"][0]
        rs, r_rs = AT["rs"][0]
        qws, r_qws = AT["qws"][0]
        sq_rot = Rot(AT["sq"])
        base = 32 + 3 * h
        bR = 7
        for (which, slot_i, ranges) in (("q", 0, (2, 3, 4)), ("k", 1, (0, 1, 2, 3, 4))):
            st_, sr_ = load_w(win_d.ap()[base + slot_i])
            for ri in ranges:
                b, n, t0 = yield from inproj_group(st_, sr_, ri)
                sq, r_sq = sq_rot.next()
                if which == "q":
                    raw, r_raw, dst, r_dst, c0, wv = rawq, r_rq, qn, r_qn, t0 - QT0, qws
                else:
                    raw, r_raw, dst, r_dst, c0, wv = rawk, r_rk, kn, r_kn, t0, kw
                CP(raw[:, c0:c0 + n], ps[:, b, 0:n], rd=[], wr=[bank[b], r_raw])
                ACT(sq[:, 0:n], raw[:, c0:c0 + n], AF.Square, rd=[r_raw], wr=[r_sq])
                MM(ps[:, bR, 0:n], onesb[:], sq[:, 0:n], True, True, rd=[r_sq, r_const], wr=[bank[bR]])
                ACT(rs[:, 0:n], ps[:, bR, 0:n], AF.Ln, rd=[], wr=[bank[bR], r_rs], bias=EPS, scale=1.0 / 128)
                ACT(rs[:, 0:n], rs[:, 0:n], AF.Exp, rd=[], wr=[r_rs], scale=-0.5)
                STT(dst[:, c0:c0 + n], raw[:, c0:c0 + n], wv[:, 0:1], rs[:, 0:n], ALU.mult, ALU.mult,
                    rd=[r_raw, r_rs, r_const, r_qws], wr=[r_dst])
                yield
        st_, sr_ = load_w(win_d.ap()[base + 2])
        for ri in range(5):
            b, n, t0 = yield from inproj_group(st_, sr_, ri)
            ACT(avT[:, t0:t0 + n], ps[:, b, 0:n], AF.Copy, rd=[], wr=[bank[b], r_av])
            yield

    def key_ap(src, dil, kb, r):
        s0 = dil * 128 * kb + r
        return src[:, s0:s0 + dil * 127 + 1:dil]

    def vt_index(dil, kb, r):
        if dil == 1:
            return kb
        if dil == 4:
            return 16 + kb * 4 + r
        return 32 + r

    def attn_vtrans(h):
        avT, r_av = AT["avT"][0]
        vt, r_vt = AT["vt"][0]
        vb_rot = Rot([7, 2, 3])
        items = []
        for kb in range(16):
            items.append((1, kb, 0))
        for kb in range(4):
            for r in range(4):
                items.append((4, kb, r))
        for r in range(16):
            items.append((16, 0, r))
        for g in range(12):
            b = vb_rot.next()
            for i in range(4):
                dil, kb, r = items[4 * g + i]
                assert vt_index(dil, kb, r) == 4 * g + i
                MM(ps[:, b, i * 128:(i + 1) * 128], key_ap(avT, dil, kb, r), identb[:], True, True,
                   rd=[r_av, r_const], wr=[bank[b]])
            CP(vt[:, 4 * g:4 * g + 4, :].rearrange("p a b -> p (a b)"), ps[:, b, :], rd=[], wr=[bank[b], r_vt])
            if g % 3 == 2:
                yield

    def attn_steps(h, par):
        qn, r_qn = AT["qn"][par]
        kn, r_kn = AT["kn"][par]
        vt, r_vt = AT["vt"][0]
        rden, r_rden = AT["rden"][0]
        ex_rot = Rot(AT["ex"])
        pT_rot = Rot(AT["pT"])
        sc_rot = Rot([2, 3, 6])
        bAcc, bDen = 4, 5
        for qr in range(3):
            T0 = QT0 + 384 * qr
            macro = []
            for i in range(3):
                macro.append(("d1", T0 // 128 + i))
            i0 = T0 // 4
            i1 = i0 + 96
            a_ = i0
            while a_ < i1:
                blk = a_ // 128
                e_ = min(i1, 128 * (blk + 1))
                for (kb, typ) in ((blk - 1, 1), (blk, 0)):
                    if kb >= 0:
                        macro.append(("d4", kb, typ, a_, e_ - a_, a_ - 128 * blk))
                a_ = e_
            macro.append(("d16", T0 // 16))
            first = [True]

            def front(m):
                bS = sc_rot.next()
                ex, r_ex = ex_rot.next()
                pT, r_pT = pT_rot.next()
                pv = []
                dn = []
                if m[0] == "d1":
                    n_ = m[1]
                    W = 256
                    qc0 = 128 * n_ - QT0
                    oc0 = 128 * n_ - T0
                    for pi_, kb in enumerate((n_, n_ - 1)):
                        MM(ps[:, bS, 128 * pi_:128 * pi_ + 128], key_ap(kn, 1, kb, 0), qn[:, qc0:qc0 + 128], True, True,
                           rd=[r_kn, r_qn], wr=[bank[bS]], skip=True)
                        pv.append((slice(oc0, oc0 + 128), vt[:, vt_index(1, kb, 0), :], pT[:, 128 * pi_:128 * pi_ + 128]))
                        dn.append((slice(oc0, oc0 + 128), flagm if kb < 8 else onesb[:], pT[:, 128 * pi_:128 * pi_ + 128]))
                    idx = (h * 3 + 0) * 2
                    e_ap = etab[:, idx:idx + 2, :].rearrange("p a b -> p (a b)")
                    ex_v = ex[:, 0:W]
                    pT_v = pT[:, 0:W]
                elif m[0] == "d4":
                    _, kb, typ, a0, cnt, ecol = m
                    W = 4 * cnt
                    oc0 = 4 * a0 - T0
                    for r in range(4):
                        qc0 = 4 * a0 + r - QT0
                        MM(ps[:, bS, r:W:4], key_ap(kn, 4, kb, r), qn[:, qc0:qc0 + 4 * (cnt - 1) + 1:4], True, True,
                           rd=[r_kn, r_qn], wr=[bank[bS]], skip=True)
                        pv.append((slice(oc0 + r, oc0 + W, 4), vt[:, vt_index(4, kb, r), :], pT[:, r:W:4]))
                    dn.append((slice(oc0, oc0 + W), flagm if kb < 2 else onesb[:], pT[:, 0:W]))
                    idx = (h * 3 + 1) * 2 + typ
                    e_ap = etab[:, idx, ecol:ecol + cnt].unsqueeze(2).to_broadcast([128, cnt, 4])
                    ex_v = ex[:, 0:W].rearrange("p (i r) -> p i r", r=4)
                    pT_v = pT[:, 0:W].rearrange("p (i r) -> p i r", r=4)
                else:
                    a0 = m[1]
                    cnt = 24
                    W = 384
                    for r in range(16):
                        qc0 = 16 * a0 + r - QT0
                        MM(ps[:, bS, r:W:16], key_ap(kn, 16, 0, r), qn[:, qc0:qc0 + 16 * (cnt - 1) + 1:16], True, True,
                           rd=[r_kn, r_qn], wr=[bank[bS]], skip=True)
                        pv.append((slice(r, W, 16), vt[:, vt_index(16, 0, r), :], pT[:, r:W:16]))
                    dn.append((slice(0, W), flagmix, pT[:, 0:W]))
                    idx = (h * 3 + 2) * 2
                    e_ap = etab[:, idx, a0:a0 + cnt].unsqueeze(2).to_broadcast([128, cnt, 16])
                    ex_v = ex[:, 0:W].rearrange("p (i r) -> p i r", r=16)
                    pT_v = pT[:, 0:W].rearrange("p (i r) -> p i r", r=16)
                ACT(ex[:, 0:W], ps[:, bS, 0:W], AF.Exp, rd=[], wr=[bank[bS], r_ex])
                TT(pT_v, ex_v, e_ap, ALU.mult, rd=[r_ex, r_const], wr=[r_pT])

                def back():
                    for (osl, lhs, rhs) in pv:
                        MM(ps[:, bAcc, osl], lhs, rhs, first[0], False, rd=[r_vt, r_pT], wr=[bank[bAcc]], skip=True)
                        for (osl2, lhs2, rhs2) in (dn[:1] if first[0] else []):
                            MM(ps[:, bDen, osl2], lhs2, rhs2, True, False, rd=[r_pT, r_const], wr=[bank[bDen]], skip=True)
                            dn.pop(0)
                        first[0] = False
                    for (osl2, lhs2, rhs2) in dn:
                        MM(ps[:, bDen, osl2], lhs2, rhs2, False, False, rd=[r_pT, r_const], wr=[bank[bDen]], skip=True)
                return back

            LA = 2
            backs = []
            for mi in range(len(macro) + LA):
                if mi < len(macro):
                    backs.append(front(macro[mi]))
                if mi >= LA:
                    backs[mi - LA]()
                yield
            ACT(rden[:], ps[:, bDen, 0:384], AF.Ln, rd=[], wr=[bank[bDen], r_rden], bias=1e-18)
            ACT(rden[:], rden[:], AF.Exp, rd=[], wr=[r_rden], scale=-1.0)
            TT(mixT[:, 8 + h, 384 * qr:384 * qr + 384], ps[:, bAcc, 0:384], rden[:], ALU.mult,
               rd=[r_rden], wr=[bank[bAcc], r_mix[8 + h]])
            yield

    def run_all(g):
        for _ in g:
            pass

    run_all(hgrn_inproj(0, 0, 0))
    for h in range(8):
        par = h % 2
        run_all(hgrn_transposes(h, par))
        ada_some(4)
        if h < 7:
            _interleave(hgrn_recur(h, par), hgrn_inproj(h + 1, 1 - par, 0), 33, 64)
        else:
            run_all(hgrn_recur(h, par))
    P.barrier()
    pb_rot.items = [0, 1]
    x1T, _ = alloc("x1T", [128, 16, NX], F32, at=R1)
    r_x1 = [[Res("x1_%d_%d" % (n, t)) for t in range(3)] for n in range(16)]
    attn_alloc()
    qws, r_qws = AT["qws"][0]
    TS(qws[:], qw[:], 128.0 ** -0.5, None, ALU.mult, None, rd=[r_const], wr=[r_qws])
    run_all(attn_inproj(0, 0))
    for h in range(8):
        par = h % 2
        run_all(attn_vtrans(h))
        ada_some(4)
        if h < 7:
            _interleave(attn_steps(h, par), attn_inproj(h + 1, 1 - par), 34, 52)
        else:
            for q in range(4):
                DMA("sp", x1T[:, 4 * q:4 * q + 4, :], xT_d.ap()[:, 4 * q:4 * q + 4, X0:X0 + NX], rd=[],
                    wr=[r_x1[n][t] for n in range(4 * q, 4 * q + 4) for t in range(3)] + r_hT, key="x1_%d" % q)
            run_all(attn_steps(h, par))
    while ada_next[0] < 96:
        ada_some(4)
    if debug:
        for u in range(16):
            DMA("pool", dbg_d.ap()[:, u, :], mixT[:, u, :], rd=[r_mix[u]], wr=[], key="dbg", nofence=True)
    P.barrier()

    off = R3
    h2T, off = alloc("h2T", [128, 16, NX], BF16, at=off)
    H2_END = off
    r_h2 = [Res("h2_%d" % t) for t in range(3)]
    sqd = []
    for i in range(3):
        t_, off = alloc("sqd%d" % i, [128, 342], BF16, at=off)
        sqd.append((t_, Res("sqd%d" % i)))
    rsd, off = alloc("rsd", [128, 342], F32, at=off)
    r_rsd = Res("rsd")
    tmpd = []
    for i in range(2):
        t_, off = alloc("tmpd%d" % i, [128, 342], F32, at=off)
        tmpd.append((t_, Res("tmpd%d" % i)))
    assert off <= R3_END
    TS(a2[:], modT[:, 64:80], 1.0, None, ALU.add, None, rd=[r_mod[4]], wr=[r_der])
    TT(a2[:], a2[:], n2w[:], ALU.mult, rd=[r_const], wr=[r_der])
    pd_rot = Rot([0, 1, 2, 3])
    d_def = []
    sqd_rot = Rot(sqd)
    tmpd_rot = Rot(tmpd)
    for n in range(16):
        st_, sr_ = load_w(wout_d.ap()[n])
        for ti, (c0, c1) in enumerate(FF_R):
            b = pd_rot.next()
            w_ = c1 - c0
            mc0 = c0 + (X0 - QT0)
            for f in range(16):
                MM(ps[:, b, 0:w_], st_[:, f * 128:(f + 1) * 128], mixT[:, f, mc0:mc0 + w_], f == 0, f == 15,
                   rd=[sr_, r_mix[f]], wr=[bank[b]])
            STT(x1T[:, n, c0:c1], ps[:, b, 0:w_], modT[:, 32 + n:33 + n], x1T[:, n, c0:c1], ALU.mult, ALU.add,
                rd=[r_mod[2]], wr=[bank[b], r_x1[n][ti]])
            sq, r_sq = sqd_rot.next()
            ACT(sq[:, 0:w_], x1T[:, n, c0:c1], AF.Square, rd=[r_x1[n][ti]], wr=[r_sq])
            for fn_ in d_def:
                fn_()
            d_def.clear()
            d_def.append(lambda sq=sq, r_sq=r_sq, w_=w_, ti=ti, n=n: MM(
                ps[:, 4 + ti, 0:w_], onesb[:], sq[:, 0:w_], n == 0, n == 15, rd=[r_sq, r_const], wr=[bank[4 + ti]]))
    for fn_ in d_def:
        fn_()
    for ti, (c0, c1) in enumerate(FF_R):
        w_ = c1 - c0
        b = 4 + ti
        ACT(rsd[:, 0:w_], ps[:, b, 0:w_], AF.Ln, rd=[], wr=[bank[b], r_rsd], bias=EPS, scale=1.0 / D)
        ACT(rsd[:, 0:w_], rsd[:, 0:w_], AF.Exp, rd=[], wr=[r_rsd], scale=-0.5)
        for c in range(16):
            tb, r_tb = tmpd_rot.next()
            TT(tb[:, 0:w_], x1T[:, c, c0:c1], rsd[:, 0:w_], ALU.mult, rd=[r_x1[c][ti], r_rsd], wr=[r_tb])
            ACT(h2T[:, c, c0:c1], tb[:, 0:w_], AF.Identity, rd=[r_tb, r_der, r_mod[3]], wr=[r_h2[ti]],
                bias=modT[:, 48 + c:49 + c], scale=a2[:, c:c + 1])
        if ti == 0:
            TS(h2T[:, :, 0:2], h2T[:, :, 0:2], flag[:, 0:1], None, ALU.mult, None, rd=[r_const], wr=[r_h2[0]])
    if debug:
        DMA("sp", dbg2_d.ap(), x1T[:], rd=[r_x1[n][t] for n in range(16) for t in range(3)], wr=[], key="dbg2")
    P.fence = []
    lastD = [P.ops[e][-1] for e in ("pe", "act", "dve") if P.ops[e]]

    yT = []
    t_, e2 = alloc("yT0", [128, 11, 1024], BF16, at=R2)
    yT.append((t_, Res("yT0")))
    t_, e3 = alloc("yT1", [128, 11, 1024], BF16, at=H2_END)
    yT.append((t_, Res("yT1")))
    assert e3 <= R3_END
    off = e2
    aS, off = alloc("aS", [128, NX], F32, at=off)
    r_aS = Res("aS")
    tcv, off = alloc("tcv", [128, 1024], F32, at=off)
    r_tcv = Res("tcv")
    sil, off = alloc("sil", [128, 1024], F32, at=off)
    r_sil = Res("sil")
    for r_ in (yT[0][1], yT[1][1], r_aS, r_tcv, r_sil):
        r_.r.extend(lastD)
    assert off <= R3, (off, R3)
    pe_rot = Rot([0, 1, 2, 3])
    pw_rot = Rot([4, 5, 6, 7])
    for g in range(4):
        yt, r_yt = yT[g % 2]
        for jj in range(11):
            j = 11 * g + jj
            sa_, ra_ = load_w(wup_d.ap()[2 * j])
            sg_, rg_ = load_w(wup_d.ap()[2 * j + 1])
            for ti, (c0, c1) in enumerate(FF_R):
                b = pe_rot.next()
                w_ = c1 - c0
                for c in range(16):
                    MM(ps[:, b, 0:w_], sa_[:, c * 128:(c + 1) * 128], h2T[:, c, c0:c1], c == 0, c == 15,
                       rd=[ra_, r_h2[ti]], wr=[bank[b]])
                ACT(aS[:, c0:c1], ps[:, b, 0:w_], AF.Copy, rd=[], wr=[bank[b], r_aS])
            TS(tcv[:], aS[:, 0:1024], convw[:, 0, j:j + 1], convb[:, j:j + 1], ALU.mult, ALU.add,
               rd=[r_aS, r_const], wr=[r_tcv])
            STT(tcv[:], aS[:, 1:1025], convw[:, 1, j:j + 1], tcv[:], ALU.mult, ALU.add, rd=[r_aS, r_const], wr=[r_tcv])
            STT(tcv[:], aS[:, 2:1026], convw[:, 2, j:j + 1], tcv[:], ALU.mult, ALU.add, rd=[r_aS, r_const], wr=[r_tcv])
            ACT(sil[:], tcv[:], AF.Silu, rd=[r_tcv], wr=[r_sil])
            for i, (c0, c1) in enumerate(OWN_R):
                b = pe_rot.next()
                for c in range(16):
                    MM(ps[:, b, 0:512], sg_[:, c * 128:(c + 1) * 128], h2T[:, c, c0:c1], c == 0, c == 15,
                       rd=[rg_, r_h2[0], r_h2[1], r_h2[2]], wr=[bank[b]])
                TT(yt[:, jj, 512 * i:512 * i + 512], ps[:, b, 0:512], sil[:, 512 * i:512 * i + 512], ALU.mult,
                   rd=[r_sil], wr=[bank[b], r_yt])
        for n in range(16):
            sd_, rd_ = load_w(wdown_d.ap()[g * 16 + n], ncols=1408)
            for i, (c0, c1) in enumerate(OWN_R):
                b = pw_rot.next()
                for jj in range(11):
                    MM(ps[:, b, 0:512], sd_[:, jj * 128:(jj + 1) * 128], yt[:, jj, 512 * i:512 * i + 512],
                       jj == 0, jj == 10, rd=[rd_, r_yt], wr=[bank[b]])
                STT(x1T[:, n, c0:c1], ps[:, b, 0:512], modT[:, 80 + n:81 + n], x1T[:, n, c0:c1], ALU.mult, ALU.add,
                    rd=[r_mod[5]], wr=[bank[b], r_x1[n][0], r_x1[n][1], r_x1[n][2]])
    for q in range(4):
        DMA("sp", outT_d.ap()[:, 4 * q:4 * q + 4, :], x1T[:, 4 * q:4 * q + 4, 2:NX],
            rd=[r_x1[n][t] for n in range(4 * q, 4 * q + 4) for t in range(3)], wr=[], key="out")
    P.barrier()
    fin = Res("fin")
    P.op("sp", lambda e: e.nop(), rd=[], wr=[fin])
    P.emit(nc)
    return nc


def _const_tables():
    etab = np.zeros((128, 48, 128), np.float32)
    kj = np.arange(128)[:, None].astype(np.float64)
    qi = np.arange(128)[None, :].astype(np.float64)
    for h in range(8):
        for pi, dil in enumerate((1, 4, 16)):
            a = SLOPES[h] * dil
            cur = np.where(qi >= kj, np.exp(-a * (qi - kj)), 0.0)
            prev = np.where(kj >= qi, np.exp(-a * (qi + 128 - kj)), 0.0)
            etab[:, (h * 3 + pi) * 2 + 0, :] = cur
            etab[:, (h * 3 + pi) * 2 + 1, :] = prev
    ident = np.eye(128, dtype=np.float32)
    cm = np.zeros((128, 128), np.float32)
    for blk in (0, 64):
        s = np.arange(64)[:, None]
        t = np.arange(64)[None, :]
        cm[blk:blk + 64, blk:blk + 64] = (s <= t).astype(np.float32)
    rmask = np.ones((128, 512), np.float32)
    rmask[:, ::64] = 0.0
    return etab.reshape(128, 48 * 128), ident, cm, rmask


def _cm(a):
    return np.ascontiguousarray(a, dtype=np.float32)


def _prepare(inputs):
    x = np.asarray(inputs["x"], np.float32)
    c = np.asarray(inputs["c"], np.float32)
    w_ada = np.asarray(inputs["w_ada"], np.float32)[0]
    w_in = np.asarray(inputs["w_in"], np.float32)[0]
    w_out = np.asarray(inputs["w_out"], np.float32)[0]
    w_up = np.asarray(inputs["w_up"], np.float32)[0]
    w_down = np.asarray(inputs["w_down"], np.float32)[0]
    sh = {}
    sh["wada"] = _cm(w_ada.reshape(16, 128, 96, 128).transpose(2, 1, 0, 3)).reshape(96, 128, 2048)
    blk = w_in.reshape(16, 128, 56, 128).transpose(2, 1, 0, 3)
    order = []
    for h in range(8):
        order += [0 + h, 24 + h, 8 + h, 16 + h]
    for h in range(8):
        order += [32 + h, 40 + h, 48 + h]
    sh["win"] = _cm(blk[order]).reshape(56, 128, 2048)
    sh["wout"] = _cm(w_out.reshape(16, 128, 16, 128).transpose(2, 1, 0, 3)).reshape(16, 128, 2048)
    blk = w_up.reshape(16, 128, 88, 128).transpose(2, 1, 0, 3)
    order = []
    for j in range(44):
        order += [j, 44 + j]
    sh["wup"] = _cm(blk[order]).reshape(88, 128, 2048)
    sh["wdown"] = _cm(w_down.reshape(4, 11, 128, 16, 128).transpose(0, 3, 2, 1, 4)).reshape(64, 128, 1408)
    sh["bada"] = _cm(np.asarray(inputs["b_ada"], np.float32)[0].reshape(96, 128).T)
    sh["n1w"] = _cm(np.asarray(inputs["norm1_w"], np.float32)[0].reshape(16, 128).T)
    sh["n2w"] = _cm(np.asarray(inputs["norm2_w"], np.float32)[0].reshape(16, 128).T)
    lb = np.asarray(inputs["lb_logits"], np.float32)
    sh["lbl"] = _cm(lb.reshape(2, 8, 128).transpose(2, 0, 1)).reshape(128, 16)
    sh["hgw"] = _cm(np.asarray(inputs["hg_norm_w"], np.float32)[0].reshape(128, 1))
    sh["qw"] = _cm(np.asarray(inputs["q_norm_w"], np.float32)[0].reshape(128, 1))
    sh["kw"] = _cm(np.asarray(inputs["k_norm_w"], np.float32)[0].reshape(128, 1))
    sh["convw"] = _cm(np.asarray(inputs["conv_w"], np.float32)[0].reshape(3, 44, 128).transpose(2, 0, 1)).reshape(128, 132)
    sh["convb"] = _cm(np.asarray(inputs["conv_b"], np.float32)[0].reshape(44, 128).T)
    etab, ident, cm, rmask = _const_tables()
    sh["etab"], sh["ident"], sh["cmask"], sh["rmask"] = etab, ident, cm, rmask
    maps = []
    for core in range(8):
        b, half = core // 2, core % 2
        if half == 1:
            xl = x[b]
            fl = 1.0
        else:
            xl = np.concatenate([np.zeros((1024, D), np.float32), x[b, :1024]], axis=0)
            fl = 0.0
        m = dict(sh)
        m["xT"] = _cm(xl.T.reshape(16, 128, 2048).transpose(1, 0, 2))
        m["flag"] = np.full((128, 1), fl, np.float32)
        m["cT"] = _cm(c[b].reshape(16, 128).T)
        fm = np.ones((128, 384), np.float32)
        fm[:, 0:128] = fl
        fm[0:64, 128:256] = fl
        m["fmix"] = fm
        maps.append(m)
    return maps


_NC_CACHE = {}


def kernel(**inputs):
    debug = _DEBUG is not None
    maps = _prepare(inputs)
    if debug not in _NC_CACHE:
        _NC_CACHE[debug] = build_program(debug)
    nc = _NC_CACHE[debug]
    res = run_bass_kernel_spmd(nc, maps, core_ids=list(range(8)))
    out = np.empty((4, S, D), np.float32)
    for core in range(8):
        b, half = core // 2, core % 2
        o = res.results[core]["outT"]
        out[b, half * 1024:(half + 1) * 1024, :] = o.transpose(2, 1, 0).reshape(1024, D)
    if debug:
        _DEBUG["dbg"] = [res.results[k]["dbg"] for k in range(8)]
        _DEBUG["dbg2"] = [res.results[k]["dbg2"] for k in range(8)]
    return out
```

```python
import numpy as np
import concourse.bass as bass
import concourse.mybir as mybir
from concourse.bass_utils import run_bass_kernel_spmd

F32 = mybir.dt.float32
BF16 = mybir.dt.bfloat16
AF = mybir.ActivationFunctionType
ALU = mybir.AluOpType

D = 2048
S = 2048
DFF = 5632
NFF = 44
EPS = 1e-6
SLOPES = [2.0 ** (-8.0 * (h + 1) / 8.0) for h in range(8)]
RANGES = [(0, 512), (512, 896), (896, 1280), (1280, 1664), (1664, 2048)]
QT0 = 896
NQ = 1152
X0 = 1022
NX = 1026
FF_R = [(0, 342), (342, 684), (684, 1026)]
OWN_R = [(2, 514), (514, 1026)]

_DEBUG = None


class Res:
    __slots__ = ("w", "r", "name")

    def __init__(self, name=""):
        self.w = None
        self.r = []
        self.name = name


class Op:
    __slots__ = ("eng", "fn", "deps", "sem", "val", "sig", "pos", "dma")


class Prog:
    ENGS = ["pe", "act", "dve", "pool", "sp"]

    def __init__(self):
        self.ops = {e: [] for e in self.ENGS}
        self.fence = []
        self.dma_sems = {}

    def op(self, eng, fn, rd=(), wr=(), dma_key=None, nofence=False):
        o = Op()
        o.eng = eng
        o.fn = fn
        o.sig = False
        o.dma = dma_key
        deps = []
        for r in rd:
            if r.w is not None:
                deps.append(r.w)
        for w in wr:
            if w.w is not None:
                deps.append(w.w)
            deps.extend(w.r)
        if not nofence:
            deps.extend(self.fence)
        for r in rd:
            r.r.append(o)
        for w in wr:
            w.w = o
            w.r = []
        o.deps = [d for d in deps if d is not o]
        o.pos = len(self.ops[eng])
        self.ops[eng].append(o)
        return o

    def barrier(self):
        f = []
        for e in ["pe", "act", "dve", "sp"]:
            if self.ops[e]:
                f.append(self.ops[e][-1])
        self.fence = f

    def emit(self, nc):
        for e in self.ENGS:
            for o in self.ops[e]:
                for d in o.deps:
                    if d.dma is not None:
                        continue
                    if d.eng == o.eng:
                        if d.eng in ("pe", "pool", "sp"):
                            continue
                        if o.pos - d.pos > 3:
                            continue
                    d.sig = True
        sems = {}
        import contextlib
        stack = contextlib.ExitStack()
        for e in ["pe", "act", "dve"]:
            sems[e] = stack.enter_context(nc.semaphore("cnt_" + e))
        dkeys = []
        for e in self.ENGS:
            for o in self.ops[e]:
                if o.dma is not None and o.dma not in dkeys:
                    dkeys.append(o.dma)
        for k in dkeys:
            sems[("dma", k)] = stack.enter_context(nc.semaphore("dma_%s" % (k,)))
        for e in ["pe", "act", "dve"]:
            c = 0
            for o in self.ops[e]:
                if o.sig:
                    c += 1
                    o.sem = sems[e]
                    o.val = c
        dcnt = {}
        for e in self.ENGS:
            for o in self.ops[e]:
                if o.dma is not None:
                    dcnt[o.dma] = dcnt.get(o.dma, 0) + 16
                    o.sem = sems[("dma", o.dma)]
                    o.val = dcnt[o.dma]
        block = stack.enter_context(nc.Block())

        def run(eng_name, engobj):
            waited = {}
            lst = self.ops[eng_name]
            for o in lst:
                need = {}
                for d in o.deps:
                    if d.dma is None:
                        if d.eng == o.eng:
                            if d.eng in ("pe", "pool", "sp"):
                                continue
                            if o.pos - d.pos > 3:
                                continue
                        if not d.sig:
                            raise RuntimeError("dep not signalled")
                    key = id(d.sem)
                    if key not in need or need[key][1] < d.val:
                        need[key] = (d.sem, d.val)
                for key, (sem, val) in need.items():
                    if waited.get(key, 0) >= val:
                        continue
                    engobj.wait_ge(sem, val)
                    waited[key] = val
                ins = o.fn(engobj)
                if o.dma is not None:
                    ins.then_inc(o.sem, 16)
                elif o.sig:
                    ins.then_inc(o.sem, 1)

        @block.tensor
        def _(t):
            run("pe", t)

        @block.scalar
        def _(s):
            run("act", s)

        @block.vector
        def _(v):
            run("dve", v)

        @block.gpsimd
        def _(g):
            run("pool", g)

        @block.sync
        def _(s):
            run("sp", s)

        stack.close()


class Rot:
    def __init__(self, items):
        self.items = items
        self.i = 0

    def next(self):
        it = self.items[self.i % len(self.items)]
        self.i += 1
        return it


def _interleave(ga, gb, na, nb):
    da = db = False
    ia = ib = 0
    while not (da and db):
        take_a = (not da) and (db or ia * nb <= ib * na)
        if take_a:
            try:
                next(ga)
            except StopIteration:
                da = True
            ia += 1
        else:
            try:
                next(gb)
            except StopIteration:
                db = True
            ib += 1


def build_program(debug=False):
    nc = bass.Bass("TRN2", target_bir_lowering=False)
    P = Prog()

    def din(name, shape):
        return nc.dram_tensor(name, shape, F32, kind="ExternalInput")

    xT_d = din("xT", [128, 16, 2048])
    flag_d = din("flag", [128, 1])
    cT_d = din("cT", [128, 16])
    wada_d = din("wada", [96, 128, 2048])
    bada_d = din("bada", [128, 96])
    n1w_d = din("n1w", [128, 16])
    n2w_d = din("n2w", [128, 16])
    win_d = din("win", [56, 128, 2048])
    lbl_d = din("lbl", [128, 16])
    hgw_d = din("hgw", [128, 1])
    qw_d = din("qw", [128, 1])
    kw_d = din("kw", [128, 1])
    wout_d = din("wout", [16, 128, 2048])
    wup_d = din("wup", [88, 128, 2048])
    convw_d = din("convw", [128, 132])
    convb_d = din("convb", [128, 44])
    wdown_d = din("wdown", [64, 128, 1408])
    etab_d = din("etab", [128, 48 * 128])
    ident_d = din("ident", [128, 128])
    cmask_d = din("cmask", [128, 128])
    rmask_d = din("rmask", [128, 512])
    fmix_d = din("fmix", [128, 384])
    outT_d = nc.dram_tensor("outT", [128, 16, 1024], F32, kind="ExternalOutput")
    dbg_d = None
    if debug:
        dbg_d = nc.dram_tensor("dbg", [128, 16, NQ], F32, kind="ExternalOutput")
        dbg2_d = nc.dram_tensor("dbg2", [128, 16, NX], F32, kind="ExternalOutput")

    A0 = 18432
    cur = [A0]

    def alloc(name, shape, dt, at=None):
        nb = int(np.prod(shape[1:])) * (2 if dt == BF16 else 4)
        nb = (nb + 31) // 32 * 32
        if at is None:
            off = cur[0]
            cur[0] += nb
        else:
            off = at
        t = nc.alloc_sbuf_tensor_at(name, list(shape), dt, offset=off)
        return t, off + nb

    NSLOT = 6
    wring = []
    for k in range(NSLOT):
        t, _ = alloc("wr%d" % k, [128, 2048], BF16)
        wring.append((t, Res("wr%d" % k)))
    etab, _ = alloc("etab", [128, 48, 128], BF16)
    identb, _ = alloc("identb", [128, 128], BF16)
    cmaskb, _ = alloc("cmaskb", [128, 128], BF16)
    fmixb, _ = alloc("fmixb", [128, 384], BF16)
    onesb, _ = alloc("onesb", [128, 128], BF16)
    rmask, _ = alloc("rmask", [128, 512], F32)
    modT, _ = alloc("modT", [128, 96], F32)
    bada, _ = alloc("bada", [128, 96], F32)
    n1w, _ = alloc("n1w", [128, 16], F32)
    n2w, _ = alloc("n2w", [128, 16], F32)
    a1, _ = alloc("a1", [128, 16], F32)
    a1p, _ = alloc("a1p", [128, 16], F32)
    s1p, _ = alloc("s1p", [128, 16], F32)
    a2, _ = alloc("a2", [128, 16], F32)
    cT, _ = alloc("cT", [128, 16], F32)
    ctmp, _ = alloc("ctmp", [128, 16], F32)
    caT, _ = alloc("caT", [128, 16], BF16)
    lbl, _ = alloc("lbl", [128, 16], F32)
    lbv, _ = alloc("lbv", [128, 8], F32)
    oml, _ = alloc("oml", [128, 8], F32)
    hgw, _ = alloc("hgw", [128, 1], F32)
    qw, _ = alloc("qw", [128, 1], F32)
    kw, _ = alloc("kw", [128, 1], F32)
    flag, _ = alloc("flag", [128, 1], F32)
    convw, _ = alloc("convw", [128, 3, 44], F32)
    convb, _ = alloc("convb", [128, 44], F32)
    CONST_END = cur[0]
    R1 = A0 + 24576 + 18432
    assert CONST_END <= R1, (CONST_END, R1)
    R2 = R1 + 65664
    R3 = R2 + 36864
    R3_END = A0 + 210944
    assert R3_END <= 229376

    ps = nc.alloc_psum_tensor("ps", [128, 8, 512], F32)
    bank = [Res("bank%d" % i) for i in range(8)]

    r_const = Res("const")
    r_mod = [Res("mod%d" % i) for i in range(6)]
    r_der = Res("derived")

    def ACT(out, in_, func, rd, wr, bias=None, scale=None):
        kw_ = {}
        if bias is not None:
            kw_["bias"] = bias
        if scale is not None:
            kw_["scale"] = scale
        return P.op("act", lambda e: e.activation(out=out, in_=in_, func=func, **kw_), rd=rd, wr=wr)

    def TT(out, in0, in1, op, rd, wr):
        return P.op("dve", lambda e: e.tensor_tensor(out=out, in0=in0, in1=in1, op=op), rd=rd, wr=wr)

    def TS(out, in0, s1, s2, op0, op1, rd, wr):
        if s2 is None:
            return P.op("dve", lambda e: e.tensor_scalar(out=out, in0=in0, scalar1=s1, scalar2=None, op0=op0), rd=rd, wr=wr)
        return P.op("dve", lambda e: e.tensor_scalar(out=out, in0=in0, scalar1=s1, scalar2=s2, op0=op0, op1=op1), rd=rd, wr=wr)

    def STT(out, in0, scalar, in1, op0, op1, rd, wr):
        return P.op("dve", lambda e: e.scalar_tensor_tensor(out=out, in0=in0, scalar=scalar, in1=in1, op0=op0, op1=op1), rd=rd, wr=wr)

    def CP(out, in_, rd, wr):
        return P.op("dve", lambda e: e.tensor_copy(out=out, in_=in_), rd=rd, wr=wr)

    def MM(out, lhsT, rhs, start, stop, rd, wr, skip=False):
        if skip:
            return P.op("pe", lambda e: e.matmul(out, lhsT=lhsT, rhs=rhs, start=start, stop=stop, skip_group_check=True), rd=rd, wr=wr)
        return P.op("pe", lambda e: e.matmul(out, lhsT=lhsT, rhs=rhs, start=start, stop=stop), rd=rd, wr=wr)

    def DMA(q, out, in_, rd, wr, key, nofence=False):
        return P.op(q, lambda e: e.dma_start(out=out, in_=in_), rd=rd, wr=wr, dma_key=key, nofence=nofence)

    cst = [(modT, None)]
    for (dst, src) in [(bada, bada_d), (n1w, n1w_d), (n2w, n2w_d), (cT, cT_d), (lbl, lbl_d), (hgw, hgw_d),
                       (qw, qw_d), (kw, kw_d), (flag, flag_d), (convb, convb_d), (rmask, rmask_d)]:
        DMA("sp", dst[:], src.ap(), rd=[], wr=[r_const], key="const")
    DMA("sp", convw[:].rearrange("p a b -> p (a b)"), convw_d.ap(), rd=[], wr=[r_const], key="const")
    for (dst, src) in [(identb, ident_d), (cmaskb, cmask_d), (fmixb, fmix_d)]:
        DMA("pool", dst[:], src.ap(), rd=[], wr=[r_const], key="constc")
    for q in range(4):
        DMA("pool", etab[:, 12 * q:12 * q + 12, :].rearrange("p a b -> p (a b)"),
            etab_d.ap()[:, 12 * q * 128:(12 * q + 12) * 128], rd=[], wr=[r_const], key="constc")
    P.op("dve", lambda e: e.memset(onesb[:], 1.0), rd=[], wr=[r_const])
    flagm = fmixb[:, 0:128]
    flagmix = fmixb[:, 128:256]

    wslot = [0]

    def load_w(src_ap, ncols=2048):
        t, r = wring[wslot[0] % NSLOT]
        wslot[0] += 1
        DMA("pool", t[:, 0:ncols], src_ap, rd=[], wr=[r], key="w%d" % ((wslot[0] - 1) % NSLOT), nofence=True)
        return t, r

    ACT(ctmp[:], cT[:], AF.Exp, rd=[r_const], wr=[r_der], scale=-1.0)
    ACT(ctmp[:], ctmp[:], AF.Ln, rd=[], wr=[r_der], bias=1.0)
    ACT(ctmp[:], ctmp[:], AF.Exp, rd=[], wr=[r_der], scale=-1.0)
    TT(caT[:], cT[:], ctmp[:], ALU.mult, rd=[r_const], wr=[r_der])
    TT(lbv[:], lbl[:, 8:16], lbl[:, 0:8], ALU.subtract, rd=[r_const], wr=[r_der])
    ACT(lbv[:], lbv[:], AF.Exp, rd=[], wr=[r_der])
    ACT(lbv[:], lbv[:], AF.Ln, rd=[], wr=[r_der], bias=1.0)
    ACT(lbv[:], lbv[:], AF.Exp, rd=[], wr=[r_der], scale=-1.0)
    TS(oml[:], lbv[:], -1.0, 1.0, ALU.mult, ALU.add, rd=[], wr=[r_der])

    pm_rot = Rot([7])
    xstg = []

    def early_x_loads():
        for oi, ri in enumerate((2, 3, 4)):
            t0, t1 = RANGES[ri]
            st, r_st = xstg[oi]
            for q in range(4):
                DMA("sp", st[:, 4 * q:4 * q + 4, 0:t1 - t0], xT_d.ap()[:, 4 * q:4 * q + 4, t0:t1], rd=[], wr=[r_st],
                    key="st%d" % oi)

    def ada_slot(j):
        t, r = load_w(wada_d.ap()[j])
        b = 7
        for c in range(16):
            MM(ps[:, b, 0:1], t[:, c * 128:(c + 1) * 128], caT[:, c:c + 1], c == 0, c == 15,
               rd=[r, r_der], wr=[bank[b]])
        TT(modT[:, j:j + 1], ps[:, b, 0:1], bada[:, j:j + 1], ALU.add, rd=[r_const], wr=[bank[b], r_mod[j // 16]])

    hT, _ = alloc("hT", [128, 16, 2048], BF16, at=R1)
    r_hT = [Res("hT%d" % i) for i in range(5)]
    stg = []
    t_, _ = alloc("stA", [128, 16, 512], F32, at=R2)
    stg.append((t_, Res("stA")))
    off = R3
    t_, off = alloc("stB", [128, 16, 512], F32, at=off)
    stg.append((t_, Res("stB")))
    t_, off = alloc("stC", [128, 16, 384], F32, at=off)
    stg.append((t_, Res("stC")))
    sqb = []
    for i in range(2):
        t_, off = alloc("sqb%d" % i, [128, 512], BF16, at=off)
        sqb.append((t_, Res("sqb%d" % i)))
    rstdB, off = alloc("rstdB", [128, 512], F32, at=off)
    r_rstd = Res("rstd")
    assert off <= R3_END
    sq_rot = Rot(sqb)
    xstg.extend(stg)
    early_x_loads()
    r_hx = [Res("hx%d" % i) for i in range(5)]
    def phaseB_range(oi, ri):
        t0, t1 = RANGES[ri]
        n = t1 - t0
        st, r_st = stg[oi % 3]
        b = oi % 2
        if oi >= 3:
            for q in range(4):
                DMA("sp", st[:, 4 * q:4 * q + 4, 0:n], xT_d.ap()[:, 4 * q:4 * q + 4, t0:t1], rd=[], wr=[r_st],
                    key="st%d" % (oi % 3))
        for c in range(16):
            sq, r_sq = sq_rot.next()
            ACT(sq[:, 0:n], st[:, c, 0:n], AF.Square, rd=[r_st], wr=[r_sq])
            MM(ps[:, b, 0:n], onesb[:], sq[:, 0:n], c == 0, c == 15, rd=[r_sq, r_const], wr=[bank[b]])
        ACT(rstdB[:, 0:n], ps[:, b, 0:n], AF.Ln, rd=[], wr=[bank[b], r_rstd], bias=EPS, scale=1.0 / D)
        ACT(rstdB[:, 0:n], rstdB[:, 0:n], AF.Exp, rd=[], wr=[r_rstd], scale=-0.5)
        for c in range(16):
            TT(hT[:, c, t0:t1], st[:, c, 0:n], rstdB[:, 0:n], ALU.mult, rd=[r_st, r_rstd], wr=[r_hx[ri]])
    for oi, ri in enumerate((2, 3, 4)):
        phaseB_range(oi, ri)
    for j in range(0, 6):
        ada_slot(j)
    phaseB_range(3, 0)
    for j in range(6, 12):
        ada_slot(j)
    phaseB_range(4, 1)
    for j in range(12, 32):
        ada_slot(j)
    TS(a1[:], modT[:, 16:32], 1.0, None, ALU.add, None, rd=[r_mod[1]], wr=[r_der])
    TT(a1[:], a1[:], n1w[:], ALU.mult, rd=[r_const], wr=[r_der])
    TS(a1p[:], a1[:], flag[:, 0:1], None, ALU.mult, None, rd=[r_const], wr=[r_der])
    TS(s1p[:], modT[:, 0:16], flag[:, 0:1], None, ALU.mult, None, rd=[r_mod[0], r_const], wr=[r_der])

    for c in range(16):
        if c % 2 == 0:
            ACT(hT[:, c, 0:1024], hT[:, c, 0:1024], AF.Identity, rd=[r_hx[0], r_hx[1], r_hx[2], r_der],
                wr=[r_hT[0], r_hT[1], r_hT[2]], bias=s1p[:, c:c + 1], scale=a1p[:, c:c + 1])
            TS(hT[:, c, 1024:2048], hT[:, c, 1024:2048], a1[:, c:c + 1], modT[:, c:c + 1], ALU.mult, ALU.add,
               rd=[r_hx[2], r_hx[3], r_hx[4], r_der, r_mod[0]], wr=[r_hT[2], r_hT[3], r_hT[4]])
        else:
            TS(hT[:, c, 0:1024], hT[:, c, 0:1024], a1p[:, c:c + 1], s1p[:, c:c + 1], ALU.mult, ALU.add,
               rd=[r_hx[0], r_hx[1], r_hx[2], r_der], wr=[r_hT[0], r_hT[1], r_hT[2]])
            ACT(hT[:, c, 1024:2048], hT[:, c, 1024:2048], AF.Identity, rd=[r_hx[2], r_hx[3], r_hx[4], r_der, r_mod[0]],
                wr=[r_hT[2], r_hT[3], r_hT[4]], bias=modT[:, c:c + 1], scale=a1[:, c:c + 1])
    P.barrier()

    mixT, _ = alloc("mixT", [128, 16, NQ], BF16, at=R2)
    r_mix = [Res("mix%d" % i) for i in range(16)]
    ada_next = [32]

    def ada_some(k):
        j0 = ada_next[0]
        k = min(k, 96 - j0)
        if k <= 0:
            return
        assert j0 // 16 == (j0 + k - 1) // 16
        b = 7
        for i in range(k):
            t, r = load_w(wada_d.ap()[j0 + i])
            for c in range(16):
                MM(ps[:, b, i:i + 1], t[:, c * 128:(c + 1) * 128], caT[:, c:c + 1], c == 0, c == 15,
                   rd=[r, r_der], wr=[bank[b]], skip=True)
        TT(modT[:, j0:j0 + k], ps[:, b, 0:k], bada[:, j0:j0 + k], ALU.add, rd=[r_const], wr=[bank[b], r_mod[j0 // 16]])
        ada_next[0] += k

    pb_rot = Rot([0, 1, 2])
    pending = []

    def flush_pending():
        while pending:
            pending.pop(0)()

    def inproj_group(slot_t, slot_r, ri):
        t0, t1 = RANGES[ri]
        n = t1 - t0
        b = pb_rot.next()
        for c in range(16):
            MM(ps[:, b, 0:n], slot_t[:, c * 128:(c + 1) * 128], hT[:, c, t0:t1], c == 0, c == 15,
               rd=[slot_r, r_hT[ri]], wr=[bank[b]])
            if c % 4 == 3 and c != 15:
                if pending:
                    pending.pop(0)()
                yield
        return b, n, t0

    off = R3
    H = {}

    def hal(name, shape, dt, nbuf=1):
        nonlocal off
        lst = []
        for i in range(nbuf):
            t_, off = alloc("h_%s%d" % (name, i), shape, dt, at=off)
            lst.append((t_, Res("h_%s%d" % (name, i))))
        H[name] = lst

    hal("kb", [128, 2048], BF16)
    hal("ka", [128, 18, 160], BF16, 2)
    hal("qd", [128, NQ], BF16, 2)
    hal("qa", [128, NQ], BF16, 2)
    hal("q1x", [128, 18, 32], BF16, 2)
    hal("gate", [128, NQ], BF16, 2)
    hal("decL", [128, 32], F32, 2)
    hal("vtok", [128, 16, 128], BF16)
    hal("ktok", [128, 16, 128], BF16)
    hal("qf", [128, NQ], BF16)
    hal("kkb", [128, 512], BF16)
    hal("hiT", [128, 2048], BF16)
    hal("oT", [128, 384], F32)
    hal("t1", [128, 512], F32)
    hal("t2", [128, 512], F32)
    hal("t3", [128, 512], F32)
    hal("t4", [128, 512], BF16)
    hal("t5", [128, 512], BF16)
    hal("sqh", [128, 384], BF16)
    hal("rst", [128, 384], F32)
    hal("state", [128, 128], F32)
    hal("Sb", [128, 128], BF16, 3)
    hal("At", [128, 64], BF16, 3)
    assert off <= R3_END, off - R3
    for (ka_, r_ka_) in H["ka"]:
        P.op("dve", lambda e, k_=ka_: e.memset(k_[:, :, 32:64], 0.0), rd=[], wr=[r_ka_])
        P.op("dve", lambda e, k_=ka_: e.memset(k_[:, :, 128:160], 0.0), rd=[], wr=[r_ka_])

    def sigmoid_chain(dst, src_ps, n, bank_res, rd_extra=()):
        (d, r_d) = dst
        ACT(d[:, 0:n], src_ps, AF.Exp, rd=list(rd_extra), wr=[bank_res, r_d], scale=-1.0)
        ACT(d[:, 0:n], d[:, 0:n], AF.Ln, rd=[], wr=[r_d], bias=1.0)
        ACT(d[:, 0:n], d[:, 0:n], AF.Exp, rd=[], wr=[r_d], scale=-1.0)

    def hgrn_inproj(h, par, part):
        kb, r_kb = H["kb"][0]
        ka, r_ka = H["ka"][par]
        qd, r_qd = H["qd"][par]
        qa, r_qa = H["qa"][par]
        q1x, r_q1x = H["q1x"][par]
        gate, r_gate = H["gate"][par]
        decL, r_decL = H["decL"][par]
        qf, r_qf = H["qf"][0]
        hiT, r_hi = H["hiT"][0]
        t1, r_t1 = H["t1"][0]
        t2, r_t2 = H["t2"][0]
        t3, r_t3 = H["t3"][0]
        t4, r_t4 = H["t4"][0]
        t5, r_t5 = H["t5"][0]
        kkb, r_kkb = H["kkb"][0]
        base = 4 * h
        if part == 0:
            st_, sr_ = load_w(win_d.ap()[base + 0])
        for ri in ((2, 3, 4) if part == 0 else ()):
            b, n, t0 = yield from inproj_group(st_, sr_, ri)
            qc = t0 - QT0
            ACT(t1[:, 0:n], ps[:, b, 0:n], AF.Copy, rd=[], wr=[bank[b], r_t1])
            flush_pending()
            ACT(t2[:, 0:n], t1[:, 0:n], AF.Exp, rd=[r_t1], wr=[r_t2], scale=-1.0)
            ACT(t2[:, 0:n], t2[:, 0:n], AF.Ln, rd=[], wr=[r_t2], bias=1.0)
            ACT(t2[:, 0:n], t2[:, 0:n], AF.Exp, rd=[], wr=[r_t2], scale=-1.0)
            TT(qf[:, qc:qc + n], t1[:, 0:n], t2[:, 0:n], ALU.mult, rd=[r_t1, r_t2], wr=[r_qf])
            yield
        if part == 0:
            st_, sr_ = load_w(win_d.ap()[base + 2])
        for ri in (range(5) if part == 0 else ()):
            b, n, t0 = yield from inproj_group(st_, sr_, ri)
            nch = n // 64
            n32 = n // 32
            c0 = t0 // 64
            gv = t2[:, 0:n].rearrange("p (c t) -> p c t", t=64)
            dv = t3[:, 0:n].rearrange("p (c t) -> p c t", t=64)
            kkv = kkb[:, 0:n].rearrange("p (c t) -> p c t", t=64)
            ACT(t1[:, 0:n], ps[:, b, 0:n], AF.Exp, rd=[], wr=[bank[b], r_t1], scale=-1.0)
            flush_pending()
            ACT(t2[:, 0:n], t1[:, 0:n], AF.Ln, rd=[r_t1], wr=[r_t2], bias=1.0)
            ACT(t2[:, 0:n], t2[:, 0:n], AF.Exp, rd=[], wr=[r_t2], scale=-1.0)
            ACT(t3[:, 0:n], t2[:, 0:n], AF.Ln, rd=[r_t2, r_der], wr=[r_t3],
                bias=lbv[:, h:h + 1], scale=oml[:, h:h + 1])

            def p1(n=n):
                STT(kkb[:, 0:n], t1[:, 0:n], oml[:, h:h + 1], t2[:, 0:n], ALU.mult, ALU.mult,
                    rd=[r_t1, r_t2, r_der], wr=[r_kkb])
                P.op("dve", lambda e, o_=t2[:, 0:n], d0=rmask[:, 0:n], d1=t3[:, 0:n]: e.tensor_tensor_scan(
                    out=o_, data0=d0, data1=d1, initial=0.0, op0=ALU.mult, op1=ALU.add),
                    rd=[r_t3, r_const], wr=[r_t2])

            def p2(n=n, nch=nch, c0=c0, gv=gv, dv=dv, t0=t0):
                ACT(decL[:, c0:c0 + nch], gv[:, :, 63], AF.Exp, rd=[r_t2], wr=[r_decL])
                TT(dv, gv, gv[:, :, 63:64].to_broadcast([128, nch, 64]), ALU.subtract, rd=[r_t2], wr=[r_t3])
                ACT(t4[:, 0:n], t3[:, 0:n], AF.Exp, rd=[r_t3], wr=[r_t4], scale=-1.0)
                TT(kb[:, t0:t0 + n], kkb[:, 0:n], t4[:, 0:n], ALU.mult, rd=[r_kkb, r_t4], wr=[r_kb])

            pieces = [p1, p2]
            if ri >= 2:
                qc = t0 - QT0
                oc0 = qc // 64
                qfv = qf[:, qc:qc + n].rearrange("p (c t) -> p c t", t=64)
                g32 = t2[:, 0:n].rearrange("p (c t) -> p c t", t=32)
                d32 = t3[:, 0:n].rearrange("p (c t) -> p c t", t=32)

                def p3(n=n, qc=qc, n32=n32, g32=g32, d32=d32):
                    ACT(t5[:, 0:n], t2[:, 0:n], AF.Exp, rd=[r_t2], wr=[r_t5])
                    TT(qa[:, qc:qc + n], qf[:, qc:qc + n], t5[:, 0:n], ALU.mult, rd=[r_t5, r_qf], wr=[r_qa])
                    TT(d32, g32, g32[:, :, 15:16].to_broadcast([128, n32, 32]), ALU.subtract, rd=[r_t2], wr=[r_t3])

                def p4(n=n, qc=qc, oc0=oc0, nch=nch):
                    ACT(t4[:, 0:n], t3[:, 0:n], AF.Exp, rd=[r_t3], wr=[r_t4])
                    TT(qd[:, qc:qc + n], qf[:, qc:qc + n], t4[:, 0:n], ALU.mult, rd=[r_t4, r_qf], wr=[r_qd])
                    ACT(t5[:, 0:n], t3[:, 0:n], AF.Exp, rd=[r_t3], wr=[r_t5], scale=-1.0)
                    ka_diag = ka[:, oc0:oc0 + nch, 0:128].rearrange("p c (two x) -> p c two x", two=2)[:, :, :, 0:32]
                    TT(ka_diag, kkb[:, 0:n].rearrange("p (c two x) -> p c two x", two=2, x=32),
                       t5[:, 0:n].rearrange("p (c two x) -> p c two x", two=2, x=32), ALU.mult,
                       rd=[r_kkb, r_t5], wr=[r_ka])

                def p5(n=n, oc0=oc0, nch=nch, gv=gv, dv=dv, qfv=qfv, kkv=kkv):
                    TT(dv, gv, gv[:, :, 31:32].to_broadcast([128, nch, 64]), ALU.subtract, rd=[r_t2], wr=[r_t3])
                    h4 = t4[:, 0:n // 2].rearrange("p (c t) -> p c t", t=32)
                    h5 = t5[:, 0:n // 2].rearrange("p (c t) -> p c t", t=32)
                    ACT(h4, dv[:, :, 32:64], AF.Exp, rd=[r_t3], wr=[r_t4])
                    TT(q1x[:, oc0:oc0 + nch, :], qfv[:, :, 32:64], h4, ALU.mult, rd=[r_t4, r_qf], wr=[r_q1x])
                    ACT(h5, dv[:, :, 0:32], AF.Exp, rd=[r_t3], wr=[r_t5], scale=-1.0)
                    TT(ka[:, oc0:oc0 + nch, 96:128], kkv[:, :, 0:32], h5, ALU.mult, rd=[r_kkb, r_t5], wr=[r_ka])

                pieces += [p3, p4, p5]
            pending.extend(pieces)
            yield
        if part == 0:
            st_, sr_ = load_w(win_d.ap()[base + 1])
        for ri in ((2, 3, 4) if part == 0 else ()):
            b, n, t0 = yield from inproj_group(st_, sr_, ri)
            qc = t0 - QT0
            ACT(t1[:, 0:n], ps[:, b, 0:n], AF.Copy, rd=[], wr=[bank[b], r_t1])
            flush_pending()
            ACT(t2[:, 0:n], t1[:, 0:n], AF.Exp, rd=[r_t1], wr=[r_t2], scale=-1.0)
            ACT(t2[:, 0:n], t2[:, 0:n], AF.Ln, rd=[], wr=[r_t2], bias=1.0)
            ACT(t2[:, 0:n], t2[:, 0:n], AF.Exp, rd=[], wr=[r_t2], scale=-1.0)
            TT(gate[:, qc:qc + n], t1[:, 0:n], t2[:, 0:n], ALU.mult, rd=[r_t1, r_t2], wr=[r_gate])
            yield
        if part == 0:
            st_, sr_ = load_w(win_d.ap()[base + 3])
        for ri in (range(5) if part == 0 else ()):
            b, n, t0 = yield from inproj_group(st_, sr_, ri)
            ACT(hiT[:, t0:t0 + n], ps[:, b, 0:n], AF.Copy, rd=[], wr=[bank[b], r_hi])
            yield
        flush_pending()
    def hgrn_transposes(h, par):
        kb, r_kb = H["kb"][0]
        hiT, r_hi = H["hiT"][0]
        vtok, r_v = H["vtok"][0]
        ktok, r_k = H["ktok"][0]
        tb_rot = Rot([3, 4])
        for (src, r_src, dst, r_dst) in ((kb, r_kb, ktok, r_k), (hiT, r_hi, vtok, r_v)):
            for g in range(4):
                b = tb_rot.next()
                for i in range(4):
                    tl = 4 * g + i
                    MM(ps[:, b, i * 128:(i + 1) * 128], src[:, tl * 128:(tl + 1) * 128], identb[:], True, True,
                       rd=[r_src, r_const], wr=[bank[b]])
                CP(dst[:, 4 * g:4 * g + 4, :].rearrange("p a b -> p (a b)"), ps[:, b, :], rd=[], wr=[bank[b], r_dst])
            yield

    def hgrn_recur(h, par):
        ka, r_ka = H["ka"][par]
        qd, r_qd = H["qd"][par]
        qa, r_qa = H["qa"][par]
        q1x, r_q1x = H["q1x"][par]
        gate, r_gate = H["gate"][par]
        decL, r_decL = H["decL"][par]
        vtok, r_v = H["vtok"][0]
        ktok, r_k = H["ktok"][0]
        oT, r_oT = H["oT"][0]
        sqh, r_sqh = H["sqh"][0]
        rst, r_rst = H["rst"][0]
        state, r_state = H["state"][0]
        bEO = [3, 4]
        bUU = [6, 7]
        bO, bR = 5, 6
        P.op("dve", lambda e: e.memset(state[:], 0.0), rd=[], wr=[r_state])

        def emit_A(ci):
            oc = ci - 14
            qc = ci * 64 - QT0
            pb_ = 64 * (ci % 2)
            bA = bEO[ci % 2]
            At, r_At = H["At"][ci % 3]
            MM(ps[pb_:pb_ + 64, bA, 0:32], ka[:, oc, 0:64], qd[:, qc:qc + 32], True, True,
               rd=[r_ka, r_qd], wr=[bank[bA]], skip=True)
            MM(ps[pb_:pb_ + 64, bA, 32:64], ka[:, oc, 32:96], qd[:, qc + 32:qc + 64], True, False,
               rd=[r_ka, r_qd], wr=[bank[bA]], skip=True)
            MM(ps[pb_:pb_ + 64, bA, 32:64], ka[:, oc, 96:160], q1x[:, oc, :], False, True,
               rd=[r_ka, r_q1x], wr=[bank[bA]], skip=True)
            TT(At[pb_:pb_ + 64, 0:64], ps[pb_:pb_ + 64, bA, 0:64], cmaskb[pb_:pb_ + 64, pb_:pb_ + 64],
               ALU.mult, rd=[r_const], wr=[bank[bA], r_At])

        def emit_o(ci):
            tile_i = ci // 2
            pb_ = 64 * (ci % 2)
            qc = ci * 64 - QT0
            rloc = (qc % 384)
            At, r_At = H["At"][ci % 3]
            Sb, r_Sb = H["Sb"][ci % 3]
            MM(ps[:, bO, rloc:rloc + 64], Sb[:], qa[:, qc:qc + 64], True, False, rd=[r_Sb, r_qa], wr=[bank[bO]])
            MM(ps[:, bO, rloc:rloc + 64], vtok[pb_:pb_ + 64, tile_i, :], At[pb_:pb_ + 64, 0:64], False, True,
               rd=[r_v, r_At], wr=[bank[bO]])
            if (ci * 64 - QT0) % 384 == 320:
                q0 = (ci * 64 - QT0) - 320
                ACT(sqh[:], ps[:, bO, 0:384], AF.Square, rd=[], wr=[bank[bO], r_sqh])
                ACT(oT[:], ps[:, bO, 0:384], AF.Copy, rd=[], wr=[bank[bO], r_oT])
                MM(ps[:, bR, 0:384], onesb[:], sqh[:], True, True, rd=[r_sqh, r_const], wr=[bank[bR]])
                ACT(rst[:], ps[:, bR, 0:384], AF.Ln, rd=[], wr=[bank[bR], r_rst], bias=EPS, scale=1.0 / 128)
                ACT(rst[:], rst[:], AF.Exp, rd=[], wr=[r_rst], scale=-0.5)
                STT(oT[:], oT[:], hgw[:, 0:1], rst[:], ALU.mult, ALU.mult, rd=[r_rst, r_const], wr=[r_oT])
                TT(mixT[:, h, q0:q0 + 384], oT[:], gate[:, q0:q0 + 384], ALU.mult, rd=[r_oT, r_gate], wr=[r_mix[h]])

        for ci in range(33):
            tile_i = ci // 2
            pb_ = 64 * (ci % 2)
            if ci == 13:
                emit_A(14)
            if ci + 1 <= 31 and ci + 1 >= 15:
                emit_A(ci + 1)
            if ci < 31:
                bU = bUU[ci % 2]
                MM(ps[:, bU, 384:512], ktok[pb_:pb_ + 64, tile_i, :], vtok[pb_:pb_ + 64, tile_i, :], True, True,
                   rd=[r_k, r_v], wr=[bank[bU]], skip=True)
            if 14 <= ci - 1 <= 31:
                emit_o(ci - 1)
            if ci < 31:
                STT(state[:], state[:], decL[:, ci:ci + 1], ps[:, bU, 384:512], ALU.mult, ALU.add,
                    rd=[r_decL], wr=[bank[bU], r_state])
                if ci + 1 >= 14:
                    Sbn, r_Sbn = H["Sb"][(ci + 1) % 3]
                    CP(Sbn[:], state[:], rd=[r_state], wr=[r_Sbn])
            yield

    AT = {}

    def attn_alloc():
        nonlocal off
        off = R3

        def aal(name, shape, dt, nbuf=1):
            nonlocal off
            lst = []
            for i in range(nbuf):
                t_, off = alloc("a_%s%d" % (name, i), shape, dt, at=off)
                lst.append((t_, Res("a_%s%d" % (name, i))))
            AT[name] = lst

        aal("qn", [128, NQ], BF16, 2)
        aal("kn", [128, 2048], BF16, 2)
        aal("rawq", [128, NQ], F32)
        aal("rawk", [128, 2048], F32)
        aal("avT", [128, 2048], BF16)
        aal("vt", [128, 48, 128], BF16)
        aal("sq", [128, 512], BF16, 2)
        aal("rs", [128, 512], F32)
        aal("ex", [128, 512], BF16, 3)
        aal("pT", [128, 512], BF16, 3)
        aal("rden", [128, 384], F32)
        aal("qws", [128, 1], F32)
        assert off <= R3_END, off - R3

    def attn_inproj(h, par):
        qn, r_qn = AT["qn"][par]
        kn, r_kn = AT["kn"][par]
        rawq, r_rq = AT["rawq"][0]
        rawk, r_rk = AT["rawk"][0]
        avT, r_av = AT["avT"][0]
        rs, r_rs = AT["rs"][0]
        qws, r_qws = AT["qws"][0]
        sq_rot = Rot(AT["sq"])
        base = 32 + 3 * h
        bR = 7
        for (which, slot_i, ranges) in (("q", 0, (2, 3, 4)), ("k", 1, (0, 1, 2, 3, 4))):
            st_, sr_ = load_w(win_d.ap()[base + slot_i])
            for ri in ranges:
                b, n, t0 = yield from inproj_group(st_, sr_, ri)
                sq, r_sq = sq_rot.next()
                if which == "q":
                    raw, r_raw, dst, r_dst, c0, wv = rawq, r_rq, qn, r_qn, t0 - QT0, qws
                else:
                    raw, r_raw, dst, r_dst, c0, wv = rawk, r_rk, kn, r_kn, t0, kw
                ACT(sq[:, 0:n], ps[:, b, 0:n], AF.Square, rd=[], wr=[bank[b], r_sq])
                CP(raw[:, c0:c0 + n], ps[:, b, 0:n], rd=[], wr=[bank[b], r_raw])
                MM(ps[:, bR, 0:n], onesb[:], sq[:, 0:n], True, True, rd=[r_sq, r_const], wr=[bank[bR]])
                ACT(rs[:, 0:n], ps[:, bR, 0:n], AF.Ln, rd=[], wr=[bank[bR], r_rs], bias=EPS, scale=1.0 / 128)
                ACT(rs[:, 0:n], rs[:, 0:n], AF.Exp, rd=[], wr=[r_rs], scale=-0.5)
                STT(dst[:, c0:c0 + n], raw[:, c0:c0 + n], wv[:, 0:1], rs[:, 0:n], ALU.mult, ALU.mult,
                    rd=[r_raw, r_rs, r_const, r_qws], wr=[r_dst])
                yield
        st_, sr_ = load_w(win_d.ap()[base + 2])
        for ri in range(5):
            b, n, t0 = yield from inproj_group(st_, sr_, ri)
            ACT(avT[:, t0:t0 + n], ps[:, b, 0:n], AF.Copy, rd=[], wr=[bank[b], r_av])
            yield

    def key_ap(src, dil, kb, r):
        s0 = dil * 128 * kb + r
        return src[:, s0:s0 + dil * 127 + 1:dil]

    def vt_index(dil, kb, r):
        if dil == 1:
            return kb
        if dil == 4:
            return 16 + kb * 4 + r
        return 32 + r

    def attn_vtrans(h):
        avT, r_av = AT["avT"][0]
        vt, r_vt = AT["vt"][0]
        vb_rot = Rot([7, 2, 3])
        items = []
        for kb in range(16):
            items.append((1, kb, 0))
        for kb in range(4):
            for r in range(4):
                items.append((4, kb, r))
        for r in range(16):
            items.append((16, 0, r))
        for g in range(12):
            b = vb_rot.next()
            for i in range(4):
                dil, kb, r = items[4 * g + i]
                assert vt_index(dil, kb, r) == 4 * g + i
                MM(ps[:, b, i * 128:(i + 1) * 128], key_ap(avT, dil, kb, r), identb[:], True, True,
                   rd=[r_av, r_const], wr=[bank[b]])
            CP(vt[:, 4 * g:4 * g + 4, :].rearrange("p a b -> p (a b)"), ps[:, b, :], rd=[], wr=[bank[b], r_vt])
            if g % 3 == 2:
                yield

    def attn_steps(h, par):
        qn, r_qn = AT["qn"][par]
        kn, r_kn = AT["kn"][par]
        vt, r_vt = AT["vt"][0]
        rden, r_rden = AT["rden"][0]
        ex_rot = Rot(AT["ex"])
        pT_rot = Rot(AT["pT"])
        sc_rot = Rot([2, 3, 6])
        bAcc, bDen = 4, 5
        for qr in range(3):
            T0 = QT0 + 384 * qr
            macro = []
            for i in range(3):
                macro.append(("d1", T0 // 128 + i))
            i0 = T0 // 4
            i1 = i0 + 96
            a_ = i0
            while a_ < i1:
                blk = a_ // 128
                e_ = min(i1, 128 * (blk + 1))
                for (kb, typ) in ((blk - 1, 1), (blk, 0)):
                    if kb >= 0:
                        macro.append(("d4", kb, typ, a_, e_ - a_, a_ - 128 * blk))
                a_ = e_
            macro.append(("d16", T0 // 16))
            first = [True]

            def front(m):
                bS = sc_rot.next()
                ex, r_ex = ex_rot.next()
                pT, r_pT = pT_rot.next()
                pv = []
                dn = []
                if m[0] == "d1":
                    n_ = m[1]
                    W = 256
                    qc0 = 128 * n_ - QT0
                    oc0 = 128 * n_ - T0
                    for pi_, kb in enumerate((n_, n_ - 1)):
                        MM(ps[:, bS, 128 * pi_:128 * pi_ + 128], key_ap(kn, 1, kb, 0), qn[:, qc0:qc0 + 128], True, True,
                           rd=[r_kn, r_qn], wr=[bank[bS]], skip=True)
                        pv.append((slice(oc0, oc0 + 128), vt[:, vt_index(1, kb, 0), :], pT[:, 128 * pi_:128 * pi_ + 128]))
                        dn.append((slice(oc0, oc0 + 128), flagm if kb < 8 else onesb[:], pT[:, 128 * pi_:128 * pi_ + 128]))
                    idx = (h * 3 + 0) * 2
                    e_ap = etab[:, idx:idx + 2, :].rearrange("p a b -> p (a b)")
                    ex_v = ex[:, 0:W]
                    pT_v = pT[:, 0:W]
                elif m[0] == "d4":
                    _, kb, typ, a0, cnt, ecol = m
                    W = 4 * cnt
                    oc0 = 4 * a0 - T0
                    for r in range(4):
                        qc0 = 4 * a0 + r - QT0
                        MM(ps[:, bS, r:W:4], key_ap(kn, 4, kb, r), qn[:, qc0:qc0 + 4 * (cnt - 1) + 1:4], True, True,
                           rd=[r_kn, r_qn], wr=[bank[bS]], skip=True)
                        pv.append((slice(oc0 + r, oc0 + W, 4), vt[:, vt_index(4, kb, r), :], pT[:, r:W:4]))
                    dn.append((slice(oc0, oc0 + W), flagm if kb < 2 else onesb[:], pT[:, 0:W]))
                    idx = (h * 3 + 1) * 2 + typ
                    e_ap = etab[:, idx, ecol:ecol + cnt].unsqueeze(2).to_broadcast([128, cnt, 4])
                    ex_v = ex[:, 0:W].rearrange("p (i r) -> p i r", r=4)
                    pT_v = pT[:, 0:W].rearrange("p (i r) -> p i r", r=4)
                else:
                    a0 = m[1]
                    cnt = 24
                    W = 384
                    for r in range(16):
                        qc0 = 16 * a0 + r - QT0
                        MM(ps[:, bS, r:W:16], key_ap(kn, 16, 0, r), qn[:, qc0:qc0 + 16 * (cnt - 1) + 1:16], True, True,
                           rd=[r_kn, r_qn], wr=[bank[bS]], skip=True)
                        pv.append((slice(r, W, 16), vt[:, vt_index(16, 0, r), :], pT[:, r:W:16]))
                    dn.append((slice(0, W), flagmix, pT[:, 0:W]))
                    idx = (h * 3 + 2) * 2
                    e_ap = etab[:, idx, a0:a0 + cnt].unsqueeze(2).to_broadcast([128, cnt, 16])
                    ex_v = ex[:, 0:W].rearrange("p (i r) -> p i r", r=16)
                    pT_v = pT[:, 0:W].rearrange("p (i r) -> p i r", r=16)
                ACT(ex[:, 0:W], ps[:, bS, 0:W], AF.Exp, rd=[], wr=[bank[bS], r_ex])
                TT(pT_v, ex_v, e_ap, ALU.mult, rd=[r_ex, r_const], wr=[r_pT])

                def back():
                    for (osl, lhs, rhs) in pv:
                        MM(ps[:, bAcc, osl], lhs, rhs, first[0], False, rd=[r_vt, r_pT], wr=[bank[bAcc]], skip=True)
                        for (osl2, lhs2, rhs2) in (dn[:1] if first[0] else []):
                            MM(ps[:, bDen, osl2], lhs2, rhs2, True, False, rd=[r_pT, r_const], wr=[bank[bDen]], skip=True)
                            dn.pop(0)
                        first[0] = False
                    for (osl2, lhs2, rhs2) in dn:
                        MM(ps[:, bDen, osl2], lhs2, rhs2, False, False, rd=[r_pT, r_const], wr=[bank[bDen]], skip=True)
                return back

            LA = 2
            backs = []
            for mi in range(len(macro) + LA):
                if mi < len(macro):
                    backs.append(front(macro[mi]))
                if mi >= LA:
                    backs[mi - LA]()
                yield
            ACT(rden[:], ps[:, bDen, 0:384], AF.Ln, rd=[], wr=[bank[bDen], r_rden], bias=1e-18)
            ACT(rden[:], rden[:], AF.Exp, rd=[], wr=[r_rden], scale=-1.0)
            TT(mixT[:, 8 + h, 384 * qr:384 * qr + 384], ps[:, bAcc, 0:384], rden[:], ALU.mult,
               rd=[r_rden], wr=[bank[bAcc], r_mix[8 + h]])
            yield

    def run_all(g):
        for _ in g:
            pass

    run_all(hgrn_inproj(0, 0, 0))
    for h in range(8):
        par = h % 2
        run_all(hgrn_transposes(h, par))
        ada_some(4)
        if h < 7:
            _interleave(hgrn_recur(h, par), hgrn_inproj(h + 1, 1 - par, 0), 33, 64)
        else:
            run_all(hgrn_recur(h, par))
    P.barrier()
    pb_rot.items = [0, 1]
    x1T, _ = alloc("x1T", [128, 16, NX], F32, at=R1)
    r_x1 = [[Res("x1_%d_%d" % (n, t)) for t in range(3)] for n in range(16)]
    attn_alloc()
    qws, r_qws = AT["qws"][0]
    TS(qws[:], qw[:], 128.0 ** -0.5, None, ALU.mult, None, rd=[r_const], wr=[r_qws])
    run_all(attn_inproj(0, 0))
    for h in range(8):
        par = h % 2
        run_all(attn_vtrans(h))
        ada_some(4)
        if h < 7:
            _interleave(attn_steps(h, par), attn_inproj(h + 1, 1 - par), 34, 52)
        else:
            for q in range(4):
                DMA("sp", x1T[:, 4 * q:4 * q + 4, :], xT_d.ap()[:, 4 * q:4 * q + 4, X0:X0 + NX], rd=[],
                    wr=[r_x1[n][t] for n in range(4 * q, 4 * q + 4) for t in range(3)] + r_hT, key="x1_%d" % q)
            run_all(attn_steps(h, par))
    while ada_next[0] < 96:
        ada_some(4)
    if debug:
        for u in range(16):
            DMA("pool", dbg_d.ap()[:, u, :], mixT[:, u, :], rd=[r_mix[u]], wr=[], key="dbg", nofence=True)
    P.barrier()

    off = R3
    h2T, off = alloc("h2T", [128, 16, NX], BF16, at=off)
    H2_END = off
    r_h2 = [Res("h2_%d" % t) for t in range(3)]
    sqd = []
    for i in range(3):
        t_, off = alloc("sqd%d" % i, [128, 342], BF16, at=off)
        sqd.append((t_, Res("sqd%d" % i)))
    rsd, off = alloc("rsd", [128, 342], F32, at=off)
    r_rsd = Res("rsd")
    tmpd = []
    for i in range(2):
        t_, off = alloc("tmpd%d" % i, [128, 342], F32, at=off)
        tmpd.append((t_, Res("tmpd%d" % i)))
    assert off <= R3_END
    TS(a2[:], modT[:, 64:80], 1.0, None, ALU.add, None, rd=[r_mod[4]], wr=[r_der])
    TT(a2[:], a2[:], n2w[:], ALU.mult, rd=[r_const], wr=[r_der])
    pd_rot = Rot([0, 1, 2, 3])
    d_def = []
    sqd_rot = Rot(sqd)
    tmpd_rot = Rot(tmpd)
    for n in range(16):
        st_, sr_ = load_w(wout_d.ap()[n])
        for ti, (c0, c1) in enumerate(FF_R):
            b = pd_rot.next()
            w_ = c1 - c0
            mc0 = c0 + (X0 - QT0)
            for f in range(16):
                MM(ps[:, b, 0:w_], st_[:, f * 128:(f + 1) * 128], mixT[:, f, mc0:mc0 + w_], f == 0, f == 15,
                   rd=[sr_, r_mix[f]], wr=[bank[b]])
            STT(x1T[:, n, c0:c1], ps[:, b, 0:w_], modT[:, 32 + n:33 + n], x1T[:, n, c0:c1], ALU.mult, ALU.add,
                rd=[r_mod[2]], wr=[bank[b], r_x1[n][ti]])
            sq, r_sq = sqd_rot.next()
            ACT(sq[:, 0:w_], x1T[:, n, c0:c1], AF.Square, rd=[r_x1[n][ti]], wr=[r_sq])
            for fn_ in d_def:
                fn_()
            d_def.clear()
            d_def.append(lambda sq=sq, r_sq=r_sq, w_=w_, ti=ti, n=n: MM(
                ps[:, 4 + ti, 0:w_], onesb[:], sq[:, 0:w_], n == 0, n == 15, rd=[r_sq, r_const], wr=[bank[4 + ti]]))
    for fn_ in d_def:
        fn_()
    for ti, (c0, c1) in enumerate(FF_R):
        w_ = c1 - c0
        b = 4 + ti
        ACT(rsd[:, 0:w_], ps[:, b, 0:w_], AF.Ln, rd=[], wr=[bank[b], r_rsd], bias=EPS, scale=1.0 / D)
        ACT(rsd[:, 0:w_], rsd[:, 0:w_], AF.Exp, rd=[], wr=[r_rsd], scale=-0.5)
        for c in range(16):
            tb, r_tb = tmpd_rot.next()
            TT(tb[:, 0:w_], x1T[:, c, c0:c1], rsd[:, 0:w_], ALU.mult, rd=[r_x1[c][ti], r_rsd], wr=[r_tb])
            ACT(h2T[:, c, c0:c1], tb[:, 0:w_], AF.Identity, rd=[r_tb, r_der, r_mod[3]], wr=[r_h2[ti]],
                bias=modT[:, 48 + c:49 + c], scale=a2[:, c:c + 1])
        if ti == 0:
            TS(h2T[:, :, 0:2], h2T[:, :, 0:2], flag[:, 0:1], None, ALU.mult, None, rd=[r_const], wr=[r_h2[0]])
    if debug:
        DMA("sp", dbg2_d.ap(), x1T[:], rd=[r_x1[n][t] for n in range(16) for t in range(3)], wr=[], key="dbg2")
    P.fence = []
    lastD = [P.ops[e][-1] for e in ("pe", "act", "dve") if P.ops[e]]

    yT = []
    t_, e2 = alloc("yT0", [128, 11, 1024], BF16, at=R2)
    yT.append((t_, Res("yT0")))
    t_, e3 = alloc("yT1", [128, 11, 1024], BF16, at=H2_END)
    yT.append((t_, Res("yT1")))
    assert e3 <= R3_END
    off = e2
    aS, off = alloc("aS", [128, NX], F32, at=off)
    r_aS = Res("aS")
    tcv, off = alloc("tcv", [128, 1024], F32, at=off)
    r_tcv = Res("tcv")
    sil, off = alloc("sil", [128, 1024], F32, at=off)
    r_sil = Res("sil")
    for r_ in (yT[0][1], yT[1][1], r_aS, r_tcv, r_sil):
        r_.r.extend(lastD)
    assert off <= R3, (off, R3)
    pe_rot = Rot([0, 1, 2, 3])
    pw_rot = Rot([4, 5, 6, 7])
    for g in range(4):
        yt, r_yt = yT[g % 2]
        for jj in range(11):
            j = 11 * g + jj
            sa_, ra_ = load_w(wup_d.ap()[2 * j])
            sg_, rg_ = load_w(wup_d.ap()[2 * j + 1])
            for ti, (c0, c1) in enumerate(FF_R):
                b = pe_rot.next()
                w_ = c1 - c0
                for c in range(16):
                    MM(ps[:, b, 0:w_], sa_[:, c * 128:(c + 1) * 128], h2T[:, c, c0:c1], c == 0, c == 15,
                       rd=[ra_, r_h2[ti]], wr=[bank[b]])
                ACT(aS[:, c0:c1], ps[:, b, 0:w_], AF.Copy, rd=[], wr=[bank[b], r_aS])
            TS(tcv[:], aS[:, 0:1024], convw[:, 0, j:j + 1], convb[:, j:j + 1], ALU.mult, ALU.add,
               rd=[r_aS, r_const], wr=[r_tcv])
            STT(tcv[:], aS[:, 1:1025], convw[:, 1, j:j + 1], tcv[:], ALU.mult, ALU.add, rd=[r_aS, r_const], wr=[r_tcv])
            STT(tcv[:], aS[:, 2:1026], convw[:, 2, j:j + 1], tcv[:], ALU.mult, ALU.add, rd=[r_aS, r_const], wr=[r_tcv])
            ACT(sil[:], tcv[:], AF.Silu, rd=[r_tcv], wr=[r_sil])
            for i, (c0, c1) in enumerate(OWN_R):
                b = pe_rot.next()
                for c in range(16):
                    MM(ps[:, b, 0:512], sg_[:, c * 128:(c + 1) * 128], h2T[:, c, c0:c1], c == 0, c == 15,
                       rd=[rg_, r_h2[0], r_h2[1], r_h2[2]], wr=[bank[b]])
                TT(yt[:, jj, 512 * i:512 * i + 512], ps[:, b, 0:512], sil[:, 512 * i:512 * i + 512], ALU.mult,
                   rd=[r_sil], wr=[bank[b], r_yt])
        for n in range(16):
            sd_, rd_ = load_w(wdown_d.ap()[g * 16 + n], ncols=1408)
            for i, (c0, c1) in enumerate(OWN_R):
                b = pw_rot.next()
                for jj in range(11):
                    MM(ps[:, b, 0:512], sd_[:, jj * 128:(jj + 1) * 128], yt[:, jj, 512 * i:512 * i + 512],
                       jj == 0, jj == 10, rd=[rd_, r_yt], wr=[bank[b]])
                STT(x1T[:, n, c0:c1], ps[:, b, 0:512], modT[:, 80 + n:81 + n], x1T[:, n, c0:c1], ALU.mult, ALU.add,
                    rd=[r_mod[5]], wr=[bank[b], r_x1[n][0], r_x1[n][1], r_x1[n][2]])
    for q in range(4):
        DMA("sp", outT_d.ap()[:, 4 * q:4 * q + 4, :], x1T[:, 4 * q:4 * q + 4, 2:NX],
            rd=[r_x1[n][t] for n in range(4 * q, 4 * q + 4) for t in range(3)], wr=[], key="out")
    P.barrier()
    fin = Res("fin")
    P.op("sp", lambda e: e.nop(), rd=[], wr=[fin])
    P.emit(nc)
    return nc


def _const_tables():
    etab = np.zeros((128, 48, 128), np.float32)
    kj = np.arange(128)[:, None].astype(np.float64)
    qi = np.arange(128)[None, :].astype(np.float64)
    for h in range(8):
        for pi, dil in enumerate((1, 4, 16)):
            a = SLOPES[h] * dil
            cur = np.where(qi >= kj, np.exp(-a * (qi - kj)), 0.0)
            prev = np.where(kj >= qi, np.exp(-a * (qi + 128 - kj)), 0.0)
            etab[:, (h * 3 + pi) * 2 + 0, :] = cur
            etab[:, (h * 3 + pi) * 2 + 1, :] = prev
    ident = np.eye(128, dtype=np.float32)
    cm = np.zeros((128, 128), np.float32)
    for blk in (0, 64):
        s = np.arange(64)[:, None]
        t = np.arange(64)[None, :]
        cm[blk:blk + 64, blk:blk + 64] = (s <= t).astype(np.float32)
    rmask = np.ones((128, 512), np.float32)
    rmask[:, ::64] = 0.0
    return etab.reshape(128, 48 * 128), ident, cm, rmask


def _cm(a):
    return np.ascontiguousarray(a, dtype=np.float32)


def _prepare(inputs):
    x = np.asarray(inputs["x"], np.float32)
    c = np.asarray(inputs["c"], np.float32)
    w_ada = np.asarray(inputs["w_ada"], np.float32)[0]
    w_in = np.asarray(inputs["w_in"], np.float32)[0]
    w_out = np.asarray(inputs["w_out"], np.float32)[0]
    w_up = np.asarray(inputs["w_up"], np.float32)[0]
    w_down = np.asarray(inputs["w_down"], np.float32)[0]
    sh = {}
    sh["wada"] = _cm(w_ada.reshape(16, 128, 96, 128).transpose(2, 1, 0, 3)).reshape(96, 128, 2048)
    blk = w_in.reshape(16, 128, 56, 128).transpose(2, 1, 0, 3)
    order = []
    for h in range(8):
        order += [0 + h, 24 + h, 8 + h, 16 + h]
    for h in range(8):
        order += [32 + h, 40 + h, 48 + h]
    sh["win"] = _cm(blk[order]).reshape(56, 128, 2048)
    sh["wout"] = _cm(w_out.reshape(16, 128, 16, 128).transpose(2, 1, 0, 3)).reshape(16, 128, 2048)
    blk = w_up.reshape(16, 128, 88, 128).transpose(2, 1, 0, 3)
    order = []
    for j in range(44):
        order += [j, 44 + j]
    sh["wup"] = _cm(blk[order]).reshape(88, 128, 2048)
    sh["wdown"] = _cm(w_down.reshape(4, 11, 128, 16, 128).transpose(0, 3, 2, 1, 4)).reshape(64, 128, 1408)
    sh["bada"] = _cm(np.asarray(inputs["b_ada"], np.float32)[0].reshape(96, 128).T)
    sh["n1w"] = _cm(np.asarray(inputs["norm1_w"], np.float32)[0].reshape(16, 128).T)
    sh["n2w"] = _cm(np.asarray(inputs["norm2_w"], np.float32)[0].reshape(16, 128).T)
    lb = np.asarray(inputs["lb_logits"], np.float32)
    sh["lbl"] = _cm(lb.reshape(2, 8, 128).transpose(2, 0, 1)).reshape(128, 16)
    sh["hgw"] = _cm(np.asarray(inputs["hg_norm_w"], np.float32)[0].reshape(128, 1))
    sh["qw"] = _cm(np.asarray(inputs["q_norm_w"], np.float32)[0].reshape(128, 1))
    sh["kw"] = _cm(np.asarray(inputs["k_norm_w"], np.float32)[0].reshape(128, 1))
    sh["convw"] = _cm(np.asarray(inputs["conv_w"], np.float32)[0].reshape(3, 44, 128).transpose(2, 0, 1)).reshape(128, 132)
    sh["convb"] = _cm(np.asarray(inputs["conv_b"], np.float32)[0].reshape(44, 128).T)
    etab, ident, cm, rmask = _const_tables()
    sh["etab"], sh["ident"], sh["cmask"], sh["rmask"] = etab, ident, cm, rmask
    maps = []
    for core in range(8):
        b, half = core // 2, core % 2
        if half == 1:
            xl = x[b]
            fl = 1.0
        else:
            xl = np.concatenate([np.zeros((1024, D), np.float32), x[b, :1024]], axis=0)
            fl = 0.0
        m = dict(sh)
        m["xT"] = _cm(xl.T.reshape(16, 128, 2048).transpose(1, 0, 2))
        m["flag"] = np.full((128, 1), fl, np.float32)
        m["cT"] = _cm(c[b].reshape(16, 128).T)
        fm = np.ones((128, 384), np.float32)
        fm[:, 0:128] = fl
        fm[0:64, 128:256] = fl
        m["fmix"] = fm
        maps.append(m)
    return maps


_NC_CACHE = {}


def kernel(**inputs):
    debug = _DEBUG is not None
    maps = _prepare(inputs)
    if debug not in _NC_CACHE:
        _NC_CACHE[debug] = build_program(debug)
    nc = _NC_CACHE[debug]
    res = run_bass_kernel_spmd(nc, maps, core_ids=list(range(8)))
    out = np.empty((4, S, D), np.float32)
    for core in range(8):
        b, half = core // 2, core % 2
        o = res.results[core]["outT"]
        out[b, half * 1024:(half + 1) * 1024, :] = o.transpose(2, 1, 0).reshape(1024, D)
    if debug:
        _DEBUG["dbg"] = [res.results[k]["dbg"] for k in range(8)]
        _DEBUG["dbg2"] = [res.results[k]["dbg2"] for k in range(8)]
    return out
```
